# Optimizing a Trainium2 kernel written in Bass

```python
import jax
import jax.numpy as jnp
from jax import lax
import numpy as np

D_MODEL = 1024
BATCH = 2
SEQ = 16384
DEPTH = 2

CHUNK = 64
EPS = 1e-6
NEG_INF = -1e30

RET_HEADS = 4
RET_QK_DIM = 128
RET_V_DIM = 256
ATT_HEADS = 8
ATT_HEAD_DIM = 64
ATT_PAST_CHUNKS = 8
MAX_REL = 128
SGU_BLOCK = 128
SGU_GROUPS = 8
SGU_WIDTH = 2048
FFN_HIDDEN = 2816
CONV_WIDTH = 3

RET_QK_W = RET_HEADS * RET_QK_DIM
RET_V_W = RET_HEADS * RET_V_DIM
ATT_W = ATT_HEADS * ATT_HEAD_DIM
AB_IN_W = 2 * RET_QK_W + 2 * RET_V_W + 3 * ATT_W
AB_OUT_W = RET_V_W + ATT_W
N_EVEN = (DEPTH + 1) // 2
N_ODD = DEPTH // 2

kernel_name = "hybrid_retention_chunkattn_gmlp_convffn"


def rms_norm(x, g):
    xf = x.astype(jnp.float32)
    y = xf * lax.rsqrt(jnp.mean(xf * xf, axis=-1, keepdims=True) + EPS)
    return (y * g.astype(jnp.float32)).astype(x.dtype)


def layer_norm(x, g, b):
    xf = x.astype(jnp.float32)
    mu = jnp.mean(xf, axis=-1, keepdims=True)
    var = jnp.mean(jnp.square(xf - mu), axis=-1, keepdims=True)
    y = (xf - mu) * lax.rsqrt(var + EPS)
    return (y * g.astype(jnp.float32) + b.astype(jnp.float32)).astype(x.dtype)


def rotary(x, pos):
    half = x.shape[-1] // 2
    inv = 1.0 / (10000.0 ** jnp.linspace(0.0, 1.0, half, dtype=jnp.float32))
    ang = pos.astype(jnp.float32)[:, None] * inv[None, :]
    cos = jnp.cos(ang)[None, :, None, :]
    sin = jnp.sin(ang)[None, :, None, :]
    xf = x.astype(jnp.float32)
    x1, x2 = xf[..., :half], xf[..., half:]
    return jnp.concatenate([x1 * cos - x2 * sin, x1 * sin + x2 * cos], axis=-1)


def retention(q, k, v):
    B, T, H, dk = q.shape
    dv = v.shape[-1]
    nc = T // CHUNK
    f32 = jnp.float32
    log_g = jnp.log1p(-jnp.exp2(-5.0 - jnp.arange(H, dtype=f32)))
    qc = q.astype(f32).reshape(B, nc, CHUNK, H, dk)
    kc = (k.astype(f32) * dk ** -0.5).reshape(B, nc, CHUNK, H, dk)
    vc = v.astype(f32).reshape(B, nc, CHUNK, H, dv)
    idx = jnp.arange(CHUNK, dtype=f32)
    d_intra = jnp.exp(log_g[:, None, None] * jnp.abs(idx[:, None] - idx[None, :]))
    s = jnp.einsum('bnihd,bnjhd->bnhij', qc, kc) * d_intra
    o_intra = jnp.einsum('bnhij,bnjhe->bnihe', s, vc)
    k_dec = jnp.exp(log_g[None, :] * (CHUNK - 1 - idx)[:, None])
    q_dec = jnp.exp(log_g[None, :] * (idx + 1.0)[:, None])
    chunk_dec = jnp.exp(log_g * CHUNK)

    def step(state, inp):
        qn, kn, vn = inp
        o = jnp.einsum('bihd,bhde->bihe', qn * q_dec[None, :, :, None], state)
        state = state * chunk_dec[None, :, None, None] + jnp.einsum(
            'bjhd,bjhe->bhde', kn * k_dec[None, :, :, None], vn)
        return state, o

    init = jnp.zeros((B, H, dk, dv), f32)
    _, o_inter = lax.scan(step, init, (qc.swapaxes(0, 1), kc.swapaxes(0, 1), vc.swapaxes(0, 1)))
    o = o_intra + o_inter.swapaxes(0, 1)
    return o.reshape(B, T, H, dv)


def _chunk_attn_one_head(args):
    q, k, v, rel_bias = args
    B, T, d = q.shape
    nc = T // CHUNK
    nb = ATT_PAST_CHUNKS + 1
    qc = q.reshape(B, nc, CHUNK, d)
    pad = ((0, 0), (ATT_PAST_CHUNKS * CHUNK, 0), (0, 0))
    kp = jnp.pad(k, pad).reshape(B, nc + ATT_PAST_CHUNKS, CHUNK, d)
    vp = jnp.pad(v, pad).reshape(B, nc + ATT_PAST_CHUNKS, CHUNK, d)
    band_idx = jnp.arange(nc)[:, None] + jnp.arange(nb)[None, :]
    kb = kp[:, band_idx].reshape(B, nc, nb * CHUNK, d)
    vb = vp[:, band_idx].reshape(B, nc, nb * CHUNK, d)
    s = jnp.einsum('bnid,bnjd->bnij', qc, kb).astype(jnp.float32) * (d ** -0.5)
    qpos = jnp.arange(CHUNK) + ATT_PAST_CHUNKS * CHUNK
    kpos = jnp.arange(nb * CHUNK)
    rel = jnp.clip(qpos[:, None] - kpos[None, :], -MAX_REL, MAX_REL) + MAX_REL
    s = s + rel_bias.astype(jnp.float32)[rel][None, None]
    valid = jnp.repeat(band_idx >= ATT_PAST_CHUNKS, CHUNK, axis=1)
    s = jnp.where(valid[None, :, None, :], s, NEG_INF)
    p = jax.nn.softmax(s, axis=-1).astype(v.dtype)
    o = jnp.einsum('bnij,bnjd->bnid', p, vb)
    return o.reshape(B, T, d)


def chunk_rel_attention(q, k, v, rel_bias):
    o = lax.map(_chunk_attn_one_head,
                (q.transpose(2, 0, 1, 3), k.transpose(2, 0, 1, 3), v.transpose(2, 0, 1, 3), rel_bias))
    return o.transpose(1, 2, 0, 3)


def ab_mixer(h, w_in, w_out, rel_bias, pos):
    B, T, _ = h.shape
    z = h @ w_in
    splits = [RET_QK_W, 2 * RET_QK_W, 2 * RET_QK_W + RET_V_W, 2 * RET_QK_W + 2 * RET_V_W,
              2 * RET_QK_W + 2 * RET_V_W + ATT_W, 2 * RET_QK_W + 2 * RET_V_W + 2 * ATT_W]
    q_a, k_a, v_a, g_a, q_b, k_b, v_b = jnp.split(z, splits, axis=-1)
    q_a = rotary(q_a.reshape(B, T, RET_HEADS, RET_QK_DIM), pos)
    k_a = rotary(k_a.reshape(B, T, RET_HEADS, RET_QK_DIM), pos)
    r = retention(q_a, k_a, v_a.reshape(B, T, RET_HEADS, RET_V_DIM))
    mu = jnp.mean(r, axis=-1, keepdims=True)
    var = jnp.mean(jnp.square(r - mu), axis=-1, keepdims=True)
    r = ((r - mu) * lax.rsqrt(var + EPS)).reshape(B, T, RET_V_W)
    y_a = (jax.nn.silu(g_a.astype(jnp.float32)) * r).astype(h.dtype)
    y_b = chunk_rel_attention(q_b.reshape(B, T, ATT_HEADS, ATT_HEAD_DIM),
                              k_b.reshape(B, T, ATT_HEADS, ATT_HEAD_DIM),
                              v_b.reshape(B, T, ATT_HEADS, ATT_HEAD_DIM),
                              rel_bias).reshape(B, T, ATT_W)
    return jnp.concatenate([y_a, y_b.astype(h.dtype)], axis=-1) @ w_out


def sgu_mixer(h, w_in, ln_g, ln_b, w_s, b_s, w_out):
    B, T, _ = h.shape
    z = jax.nn.gelu(h @ w_in)
    u, v = jnp.split(z, 2, axis=-1)
    v = layer_norm(v, ln_g, ln_b)
    nb = T // SGU_BLOCK
    vg = v.reshape(B, nb, SGU_BLOCK, SGU_GROUPS, SGU_WIDTH // SGU_GROUPS)
    i = jnp.arange(SGU_BLOCK)
    mask = (i[None, :] // CHUNK) <= (i[:, None] // CHUNK)
    w = jnp.where(mask[None], w_s, jnp.zeros_like(w_s))
    mixed = jnp.einsum('gij,bnjgc->bnigc', w, vg) + b_s.T[None, None, :, :, None]
    y = u * mixed.reshape(B, T, SGU_WIDTH)
    return y @ w_out


def conv_ffn(h, w_up, conv_w, conv_b, w_down):
    z = h @ w_up
    c = z.shape[-1]
    z = lax.conv_general_dilated(z, conv_w[:, None, :].astype(z.dtype), window_strides=(1,),
                                 padding=[(CONV_WIDTH - 1, 0)],
                                 dimension_numbers=('NWC', 'WIO', 'NWC'),
                                 feature_group_count=c) + conv_b
    gate, up = jnp.split(z, 2, axis=-1)
    return (jax.nn.gelu(gate) * up) @ w_down


def setup_inputs(seed: int = 0) -> dict:
    key = jax.random.key(seed)
    ks = jax.random.split(key, 20)
    n = jax.random.normal
    f32 = jnp.float32
    nr = 2 * MAX_REL + 1
    return {
        "x": n(ks[0], (BATCH, SEQ, D_MODEL), f32),
        "attn_norm_g": 1.0 + 0.02 * n(ks[1], (DEPTH, D_MODEL), f32),
        "ffn_norm_g": 1.0 + 0.02 * n(ks[2], (DEPTH, D_MODEL), f32),
        "ab_w_in": n(ks[3], (N_EVEN, D_MODEL, AB_IN_W), f32) * D_MODEL ** -0.5,
        "ab_w_out": n(ks[4], (N_EVEN, AB_OUT_W, D_MODEL), f32) * AB_OUT_W ** -0.5,
        "ab_rel_bias": 0.1 * n(ks[5], (N_EVEN, ATT_HEADS, nr), f32),
        "c_w_in": n(ks[6], (N_ODD, D_MODEL, 2 * SGU_WIDTH), f32) * D_MODEL ** -0.5,
        "c_ln_g": 1.0 + 0.02 * n(ks[7], (N_ODD, SGU_WIDTH), f32),
        "c_ln_b": 0.02 * n(ks[8], (N_ODD, SGU_WIDTH), f32),
        "c_w_s": n(ks[9], (N_ODD, SGU_GROUPS, SGU_BLOCK, SGU_BLOCK), f32) * SGU_BLOCK ** -0.5,
        "c_b_s": 1.0 + 0.02 * n(ks[10], (N_ODD, SGU_GROUPS, SGU_BLOCK), f32),
        "c_w_out": n(ks[11], (N_ODD, SGU_WIDTH, D_MODEL), f32) * SGU_WIDTH ** -0.5,
        "ffn_w_up": n(ks[12], (DEPTH, D_MODEL, 2 * FFN_HIDDEN), f32) * D_MODEL ** -0.5,
        "ffn_conv_w": n(ks[13], (DEPTH, CONV_WIDTH, 2 * FFN_HIDDEN), f32) * CONV_WIDTH ** -0.5,
        "ffn_conv_b": 0.02 * n(ks[14], (DEPTH, 2 * FFN_HIDDEN), f32),
        "ffn_w_down": n(ks[15], (DEPTH, FFN_HIDDEN, D_MODEL), f32) * FFN_HIDDEN ** -0.5,
        "final_norm_g": 1.0 + 0.02 * n(ks[16], (D_MODEL,), f32),
    }


def reference(x, attn_norm_g, ffn_norm_g, ab_w_in, ab_w_out, ab_rel_bias, c_w_in, c_ln_g,
              c_ln_b, c_w_s, c_b_s, c_w_out, ffn_w_up, ffn_conv_w, ffn_conv_b, ffn_w_down,
              final_norm_g):
    h = x
    pos = jnp.arange(x.shape[1])
    for layer in range(DEPTH):
        hn = rms_norm(h, attn_norm_g[layer])
        i = layer // 2
        if layer % 2 == 0:
            h = h + ab_mixer(hn, ab_w_in[i], ab_w_out[i], ab_rel_bias[i], pos).astype(h.dtype)
        else:
            h = h + sgu_mixer(hn, c_w_in[i], c_ln_g[i], c_ln_b[i], c_w_s[i], c_b_s[i],
                              c_w_out[i]).astype(h.dtype)
        h = h + conv_ffn(rms_norm(h, ffn_norm_g[layer]), ffn_w_up[layer], ffn_conv_w[layer],
                         ffn_conv_b[layer], ffn_w_down[layer]).astype(h.dtype)
    return rms_norm(h, final_norm_g)
```

```python
import numpy as np
from contextlib import ExitStack
import concourse.bass as bass
import concourse.mybir as mybir
from concourse.bass_utils import run_bass_kernel_spmd

F32, BF16 = mybir.dt.float32, mybir.dt.bfloat16
AF = mybir.ActivationFunctionType
ALU = mybir.AluOpType

import os
CUT = int(os.environ.get('K_CUT', '99'))
P = 128
EPS = 1e-6
NSLOT = 4
ENGS = ("pe", "act", "dve", "pool", "sp")


def _okey(op):
    return op.dsem.key if op.dsem is not None else op.eng


class Res:
    __slots__ = ("name", "w", "r", "kids", "excl")

    def __init__(self, name="", inherit=()):
        self.name = name
        self.excl = False
        self.w = None
        self.r = {}
        self.kids = []
        stack = list(inherit)
        while stack:
            o = stack.pop()
            if o.w is not None:
                self._put(o.w)
            for q in o.r.values():
                self._put(q)
            stack.extend(o.kids)

    def _put(self, op):
        k = _okey(op)
        cur = self.r.get(k)
        if cur is None or cur.idx < op.idx:
            self.r[k] = op


class DmaSem:
    def __init__(self, sem, key):
        self.sem = sem
        self.key = key
        self.count = 0


import os


class _Stop(Exception):
    pass


CUTK = int(os.environ.get('K_CUTK', '1'))
_hits = {}


def cut(n):
    if CUT == n:
        _hits[n] = _hits.get(n, 0) + 1
        if _hits[n] >= CUTK:
            raise _Stop()


class Op:
    __slots__ = ("eng", "fn", "deps", "sig", "dsem", "dval", "waits", "clock", "need", "epoch", "key", "idx")


class Sched:
    def __init__(self, nc, stack):
        self.nc = nc
        self.stack = stack
        self.ops = []
        self.epoch = 0
        self.nsem = 0
        self.semobjs = {}

    def new_sem(self, name):
        s = self.stack.enter_context(self.nc.semaphore(name))
        self.nsem += 1
        return s

    def dma_sem(self, name):
        s = self.new_sem(name)
        d = DmaSem(s, ("dma", name))
        self.semobjs[d.key] = s
        return d

    def add(self, eng, fn, reads=(), writes=(), dsem=None, force=()):
        op = Op()
        op.eng = eng
        op.fn = fn
        op.dsem = dsem
        op.sig = 0
        op.need = False
        op.epoch = self.epoch
        op.dval = 0
        if dsem is not None:
            dsem.count += 16
            op.dval = dsem.count
        op.idx = len(self.ops)
        deps = set()
        for r in reads:
            if r.w is not None:
                deps.add(r.w)
            if r.excl:
                deps.update(o for o in r.r.values() if o.eng != eng)
        for r in writes:
            if r.w is not None:
                deps.add(r.w)
            deps.update(r.r.values())
        for r in reads:
            r.r[_okey(op)] = op
        for r in writes:
            r.w = op
            r.r = {}
        deps.discard(op)
        if eng == "pe":
            deps = {d for d in deps if not (d.eng == "pe" and d.dsem is None)}
        if dsem is not None:
            deps = {d for d in deps if d.dsem is not dsem}
        deps.update(force)
        op.deps = sorted(deps, key=lambda d: d.idx, reverse=(os.environ.get('K_ORD') == 'rev'))
        self.ops.append(op)
        return op

    def emit(self, final_waits=()):
        nc = self.nc
        for op in self.ops:
            for d in op.deps:
                if d.dsem is None:
                    d.need = True
        cnt = {}
        for op in self.ops:
            if op.dsem is None:
                k = (op.eng, op.epoch)
                if op.need:
                    cnt[k] = cnt.get(k, 0) + 1
                    op.sig = cnt[k]
                op.key = k
            else:
                op.key = op.dsem.key
        for k in cnt:
            self.semobjs[k] = self.new_sem("s_%s_%d" % k)
        clock = {e: {} for e in ENGS}
        lastsig = {}
        nwaits = 0
        for op in self.ops:
            ck = clock[op.eng]
            waits = {}
            for d in op.deps:
                key = d.key
                val = d.dval if d.dsem is not None else d.sig
                if ck.get(key, 0) >= val:
                    continue
                waits[key] = max(waits.get(key, 0), val)
                for k, v in d.clock.items():
                    if ck.get(k, 0) < v:
                        ck[k] = v
                ck[key] = max(ck.get(key, 0), val)
            op.waits = list(waits.items())
            nwaits += len(op.waits)
            snap = dict(ck)
            if op.dsem is None:
                if op.sig:
                    lastsig[op.key] = op.sig
                if op.key in lastsig:
                    snap[op.key] = max(snap.get(op.key, 0), lastsig[op.key])
            op.clock = snap
        byeng = {e: [] for e in ENGS}
        for op in self.ops:
            byeng[op.eng].append(op)
        self.stats = {e: len(v) for e, v in byeng.items()}
        self.stats["waits"] = nwaits
        self.stats["sems"] = self.nsem
        semobjs = self.semobjs

        def run(name, e):
            for op in byeng[name]:
                for k, v in op.waits:
                    e.wait_ge(semobjs[k], v)
                ins = op.fn(e)
                if op.dsem is not None:
                    ins.then_inc(op.dsem.sem, 16)
                elif op.sig:
                    ins.then_inc(semobjs[op.key], 1)
            if name == "sp":
                for d in final_waits:
                    e.wait_ge(d.sem, d.count)

        with nc.Block() as block:
            @block.tensor
            def _(e):
                run("pe", e)

            @block.scalar
            def _(e):
                run("act", e)

            @block.vector
            def _(e):
                run("dve", e)

            @block.gpsimd
            def _(e):
                run("pool", e)

            @block.sync
            def _(e):
                run("sp", e)


def _col_layout():
    lay = {}
    off = 0
    for name, n in (("g_attn0", 8), ("g_ffn0", 8), ("g_attn1", 8), ("g_ffn1", 8),
                    ("cw0", 44 * 3), ("cb0", 44), ("cw1", 44 * 3), ("cb1", 44),
                    ("ln_g", 16), ("ln_b", 16), ("kdec", 4), ("qdec", 4), ("flag", 1), ("one", 1)):
        lay[name] = (off, n)
        off += n
    return lay, off


COLS, NCOLS = _col_layout()

def _page_table():
    pages = []
    idx = {}

    def add(group, w, kc0, kcn, segs, g):
        idx.setdefault(group, []).append(len(pages))
        pages.append((w, kc0, kcn, segs, g))

    for i in range(9):
        add("l0_in", "w_in", 0, 8, [(512 * i, 512)], "g_attn0")
    for cg in range(2):
        for (k0, kn) in ((0, 8), (8, 4)):
            add("l0_out", "w_out", k0, kn, [(512 * cg, 512)], None)
    for l in range(2):
        for k in range(11):
            add("f%d_up" % l, "w_up%d" % l, 0, 8, [(256 * k, 256), (2816 + 256 * k, 256)], "g_ffn%d" % l)
        for cg in range(2):
            for (k0, kn) in ((0, 8), (8, 8), (16, 6)):
                add("f%d_dn" % l, "w_dn%d" % l, k0, kn, [(512 * cg, 512)], None)
    for i in range(8):
        add("l1_in", "c_in", 0, 8, [(512 * i, 512)], "g_attn1")
    for cg in range(2):
        for (k0, kn) in ((0, 8), (8, 8)):
            add("l1_out", "c_out", k0, kn, [(512 * cg, 512)], None)
    return pages, idx


PAGES, PIDX = _page_table()
WSHAPES = {"w_in": (1024, 4608), "w_out": (1536, 1024), "w_up0": (1024, 5632), "w_up1": (1024, 5632),
           "w_dn0": (2816, 1024), "w_dn1": (2816, 1024), "c_in": (1024, 4096), "c_out": (2048, 1024)}

ARENA_BYTES = 77312
L0_LAY = {"va": (0, 8192), "sg": (8192, 8192), "kdec": (16384, 4096), "qaT": (20480, 4096),
          "kaT": (24576, 4096), "qbT": (28672, 4096), "qa_tm": (32768, 2048), "ka_tm": (34816, 2048),
          "S_bf": (36864, 4096), "y_tm": (40960, 6144), "yT": (47104, 12288), "p32": (59392, 5120),
          "pb": (64512, 2560), "o_sb": (67072, 2048), "t_sb": (69120, 2048), "sTm": (71168, 2048),
          "rtmp": (73216, 4096)}
FFN_LAY = {"aT": (0, 22528), "acc": (22528, 8192)}
SGU_LAY = {"uT": (0, 16384), "vg": (16384, 16384), "yT2": (32768, 16384), "m4": (49152, 4096)}
PRE_LAY = {"st32": (0, 32768), "st16": (32768, 16384), "bias": (49152, 20480)}


def build_program(n_warm, n_main, stop_stage=5):
    NT = n_warm + 1 + n_main
    nc = bass.Bass("TRN2", target_bir_lowering=False)
    din = lambda n, sh: nc.dram_tensor(n, list(sh), F32, kind="ExternalInput").ap()
    x_d = din("x", (NT * 512, 1024))
    rot_d = din("rot", (NT * 512, 256))
    w_d = {n: din(n, sh) for n, sh in WSHAPES.items()}
    cols_d = din("cols", (P, NCOLS))
    gfin_d = din("gfin", (P, 1024))
    ident_d = din("ident", (P, P))
    bias_d = din("biasT", (P, 8 * 5 * 128))
    dq_d = din("dq", (P, 4 * 128))
    wsT_d = din("wsT", (P, 8 * 128))
    mask_d = din("maskT", (P, 128))
    bsb_d = din("bsb", (P, 8 * 128))
    out_d = nc.dram_tensor("out", [n_main * 512, 1024], F32, kind="ExternalOutput").ap()
    scr_d = nc.dram_tensor("wscr", [len(PAGES), P, 4096], BF16, kind="Internal").ap()

    G128 = [float((1.0 - 2.0 ** (-5 - h)) ** 128) for h in range(4)]

    with ExitStack() as st:
        S = Sched(nc, st)
        sb = lambda n, sh, dt: st.enter_context(nc.sbuf_tensor("sb_" + n, list(sh), dt))
        ps = st.enter_context(nc.psum_tensor("ps", [P, 4096], F32))

        hbuf = [sb("h%d" % b, (P, 4, 1024), F32) for b in range(2)]
        hres = [[Res("h%d_%d" % (b, s)) for s in range(4)] for b in range(2)]
        hnT = sb("hnT", (P, 8, 512), BF16)
        hnT_res = [Res("hnT%d" % s) for s in range(4)]
        hn_tm = [sb("hn_tm%d" % k, (P, 1024), BF16) for k in range(2)]
        hn_tm_res = [Res("hn_tm%d" % k) for k in range(2)]
        junk = sb("junk", (P, 1024), BF16)
        wslot = [sb("wslot%d" % k, (P, 4096), BF16) for k in range(NSLOT)]
        wslot_res = [Res("wslot%d" % k) for k in range(NSLOT)]
        wslot_sem = [S.dma_sem("wsl%d" % k) for k in range(NSLOT)]
        cols = sb("cols", (P, NCOLS), F32)
        gfin = sb("gfin", (P, 1024), F32)
        ident = sb("ident", (P, P), BF16)
        EB = sb("EB", (P, 8, 640), BF16)
        Dq = sb("Dq", (P, 4, 128), F32)
        wTm = sb("wTm", (P, 8, 128), BF16)
        Rt = sb("Rt", (P, 16, 128), F32)
        ones_bf = sb("ones_bf", (P, P), BF16)
        rot = sb("rot", (P, 4, 256), F32)
        rot_res = Res("rot")
        kT = sb("kT", (P, 4, 1024), BF16)
        kT_res = [[Res("kT%d_%d" % (hf, oc)) for oc in range(4)] for hf in range(2)]
        Vr = sb("Vr", (P, 8, 8, 65), BF16)
        V_res = [[Res("V%d_%d" % (hf, s)) for s in range(4)] for hf in range(2)]
        Sst = sb("Sst", (P, 4, 256), F32)
        S_res = [Res("S%d" % h) for h in range(4)]
        zc = [sb("zc%d" % l, (P, 44, 2), F32) for l in range(2)]
        zc_res = [[Res("zc%d_%d" % (l, c)) for c in range(44)] for l in range(2)]
        NSTAT = 12
        stat = [sb("stat%d" % i, (P, 16), F32) for i in range(NSTAT)]
        stat_res = [Res("stat%d" % i) for i in range(NSTAT)]
        stat_ptr = [0]
        arena = sb("arena", (P, ARENA_BYTES // 2), BF16)
        const_res = Res("const")
        setup_res = Res("setup")

        def colap(name, i=0, n=1):
            o, _ = COLS[name]
            return cols[:, o + i:o + i + n]

        live = []

        def carve(lay, name, dtype, shape):
            lo, nb = lay[name]
            hi = lo + nb
            olds = [r for (a, b, r) in live if a < hi and b > lo]
            keep = []
            for (a, b, r) in live:
                if a < hi and b > lo:
                    if a < lo:
                        keep.append((a, lo, r))
                    if b > hi:
                        keep.append((hi, b, r))
                else:
                    keep.append((a, b, r))
            live[:] = keep
            res = Res(name, inherit=olds)
            live.append((lo, hi, res))
            ap = arena[:, lo // 2:hi // 2]
            if dtype == F32:
                ap = ap.bitcast(F32)
            return ap, res

        def split_res(res, n):
            kids = [Res("%s_%d" % (res.name, i), inherit=[res]) for i in range(n)]
            res.kids.extend(kids)
            return kids

        psq = [Res("psb%d" % i) for i in range(8)]
        for r_ in psq:
            r_.excl = True
        psptr = [0]

        def palloc(nq=4):
            i = psptr[0]
            psptr[0] = (i + 1) % 8
            return i * 512, [psq[i]]

        load = {"act": 0.0, "dve": 0.0}

        def pick(n):
            e = "act" if load["act"] <= load["dve"] else "dve"
            return e

        def cost(eng, n):
            if eng == "act":
                load["act"] += n / 1.4 + 160
            elif eng == "dve":
                load["dve"] += n / 0.96 + 60

        def act(out, in_, func, reads, writes, n=512, **kw):
            cost("act", n)
            return S.add("act", lambda e: e.activation(out=out, in_=in_, func=func, **kw), reads, writes)

        def copy(eng, out, in_, reads, writes, n=512):
            cost(eng, n)
            if eng == "act":
                return S.add("act", lambda e: e.activation(out=out, in_=in_, func=AF.Copy), reads, writes)
            return S.add(eng, lambda e: e.tensor_copy(out=out, in_=in_), reads, writes)

        def tt(out, in0, in1, op, reads, writes, n=512, eng="dve"):
            cost(eng, n)
            return S.add(eng, lambda e: e.tensor_tensor(out=out, in0=in0, in1=in1, op=op), reads, writes)

        def ts(out, in0, s1, s2, op0, op1, reads, writes, n=64):
            cost("dve", n)
            if s2 is None:
                return S.add("dve", lambda e: e.tensor_scalar(out=out, in0=in0, scalar1=s1, scalar2=None, op0=op0), reads, writes)
            return S.add("dve", lambda e: e.tensor_scalar(out=out, in0=in0, scalar1=s1, scalar2=s2, op0=op0, op1=op1), reads, writes)

        def stt(out, in0, scalar, in1, op0, op1, reads, writes, n=512):
            cost("dve", n)
            return S.add("dve", lambda e: e.scalar_tensor_tensor(out=out, in0=in0, scalar=scalar, in1=in1, op0=op0, op1=op1), reads, writes)

        def mm(out, lhsT, rhs, start, stop, reads, writes, force=()):
            return S.add("pe", lambda e: e.matmul(out, lhsT=lhsT, rhs=rhs, start=start, stop=stop), reads, writes, force=force)

        def trp(out, in_, reads, writes):
            return S.add("pe", lambda e: e.transpose(out, in_, ident[:]), list(reads) + [setup_res], writes)

        def dma(eng, out, in_, reads, writes, dsem):
            return S.add(eng, lambda e: e.dma_start(out=out, in_=in_), reads, writes, dsem=dsem)

        def new_stat():
            i = stat_ptr[0] % NSTAT
            stat_ptr[0] += 1
            return stat[i], stat_res[i]

        if CUT >= 1:
            cst = S.dma_sem("cst")
            cops = []
            biasT, r_bias = carve(PRE_LAY, "bias", F32, None)
            for dst, src in ((cols[:], cols_d), (gfin[:], gfin_d), (Dq[:].rearrange("p a b -> p (a b)"), dq_d)):
                cops.append(dma("sp", dst, src, [], [const_res], cst))
            cops.append(dma("sp", biasT, bias_d, [], [r_bias], cst))
            a_st32, r_st32 = carve(PRE_LAY, "st32", F32, None)
            a_st16, r_st16 = carve(PRE_LAY, "st16", BF16, None)
            wsT_f = a_st32[:, 0:1024]
            mask_f = a_st32[:, 1024:1152]
            bsb_f = a_st32[:, 1152:2176]
            rs_f = a_st32[:, 2176:3200]
            identf = a_st32[:, 3200:3328]
            cops.append(dma("sp", identf, ident_d, [], [r_st32], cst))
            cops.append(dma("sp", wsT_f, wsT_d, [], [r_st32], cst))
            cops.append(dma("sp", mask_f, mask_d, [], [r_st32], cst))
            cops.append(dma("sp", bsb_f, bsb_d, [], [r_st32], cst))
            for o in cops:
                o.dval = cst.count
            copy("dve", ident[:], identf, [r_st32], [setup_res], n=128)
            S.add("dve", lambda e: e.memset(ones_bf[:], 1.0), [], [setup_res])
            for h in range(8):
                act(EB[:, h, :], biasT[:, h * 640:(h + 1) * 640], AF.Exp, [r_bias], [setup_res], n=640)
            tt(wTm[:].rearrange("p g i -> p g i"), wsT_f.rearrange("p (g i) -> p g i", g=8),
               mask_f.unsqueeze(1).broadcast_to([P, 8, 128]), ALU.mult, [r_st32], [setup_res], n=1024)
            for g in range(8):
                pc, pr = palloc(1)
                mm(ps[:, pc:pc + 128], ones_bf[:], wTm[:, g, :], True, True, [setup_res], pr)
                copy("act", rs_f[:, g * 128:(g + 1) * 128], ps[:, pc:pc + 128], pr, [r_st32], n=128)
            for c in range(16):
                g = c // 2
                stt(Rt[:, c, :], rs_f[:, g * 128:(g + 1) * 128], colap("ln_b", c), bsb_f[:, g * 128:(g + 1) * 128],
                    ALU.mult, ALU.add, [r_st32, const_res], [setup_res], n=128)
            S.add("dve", lambda e: e.memset(kT[:], 0.0), [], [r for hf in kT_res for r in hf])
            S.add("dve", lambda e: e.memset(Vr[:], 0.0), [], [r for hf in V_res for r in hf])
            S.add("dve", lambda e: e.memset(Sst[:], 0.0), [], S_res)
            for l in range(2):
                S.add("dve", lambda e, l=l: e.memset(zc[l][:], 0.0), [], zc_res[l])


        if CUT >= 2:
            st32 = [a_st32[:, k * 4096:(k + 1) * 4096] for k in range(2)]
            st16 = [a_st16[:, k * 4096:(k + 1) * 4096] for k in range(2)]
            st32_res = split_res(r_st32, 2)
            st16_res = split_res(r_st16, 2)
            ld_sem = [S.dma_sem("pld%d" % k) for k in range(2)]
            stq_sem = [S.dma_sem("pst%d" % k) for k in range(2)]
            scr_res = [Res("scr%d" % i) for i in range(len(PAGES))]
            for pi, (wn, kc0, kcn, segs, gname) in enumerate(PAGES):
                k = pi % 2
                wv = w_d[wn].rearrange("(kc p) n -> p kc n", p=P)
                s3 = st32[k].rearrange("p (kc n) -> p kc n", n=512)
                lops = []
                co = 0
                for (c0, ncol) in segs:
                    lops.append(dma("sp", s3[:, 0:kcn, co:co + ncol], wv[:, kc0:kc0 + kcn, c0:c0 + ncol], [], [st32_res[k]], ld_sem[k]))
                    co += ncol
                for o in lops:
                    o.dval = ld_sem[k].count
                d3 = st16[k].rearrange("p (kc n) -> p kc n", n=512)
                for kc in range(kcn):
                    eng = pick(512)
                    if gname is None:
                        copy(eng, d3[:, kc, :], s3[:, kc, :], [st32_res[k]], [st16_res[k]])
                    elif eng == "act":
                        act(d3[:, kc, :], s3[:, kc, :], AF.Copy, [st32_res[k], const_res], [st16_res[k]], scale=colap(gname, kc0 + kc))
                    else:
                        ts(d3[:, kc, :], s3[:, kc, :], colap(gname, kc0 + kc), None, ALU.mult, None, [st32_res[k], const_res], [st16_res[k]], n=512)
                dma("sp", scr_d[pi, :, 0:kcn * 512], st16[k][:, 0:kcn * 512], [st16_res[k]], [scr_res[pi]], stq_sem[k])


        seq = []
        for t in range(NT):
            if t < n_warm:
                pl = [PIDX["l0_in"][i] for i in (1, 2, 3)]
                if t == n_warm - 1:
                    pl += [PIDX["l0_in"][7], PIDX["l0_in"][8]]
            else:
                pl = list(PIDX["l0_in"]) + list(PIDX["l0_out"])
                if stop_stage >= 2:
                    pl += PIDX["f0_up"] + PIDX["f0_dn"]
                if stop_stage >= 3:
                    pl += PIDX["l1_in"] + PIDX["l1_out"]
                if stop_stage >= 4:
                    pl += PIDX["f1_up"] + PIDX["f1_dn"]
            seq += pl
        wstate = {"issued": 0, "used": 0}

        def w_prefetch(upto):
            while wstate["issued"] < min(upto, len(seq)):
                i = wstate["issued"]
                pi = seq[i]
                kcn = PAGES[pi][2]
                k = i % NSLOT
                dma("sp", wslot[k][:, 0:kcn * 512], scr_d[pi, :, 0:kcn * 512], [scr_res[pi]], [wslot_res[k]], wslot_sem[k])
                wstate["issued"] += 1

        def w_group(pis):
            i0 = wstate["used"]
            assert len(pis) <= NSLOT - 1
            w_prefetch(i0 + NSLOT)
            outl = []
            for m, expect in enumerate(pis):
                i = i0 + m
                assert seq[i] == expect, (i, seq[i], expect)
                k = i % NSLOT
                outl.append((wslot[k][:].rearrange("p (kc n) -> p kc n", n=512), wslot_res[k]))
            wstate["used"] += len(pis)
            return outl

        def w_next(expect):
            return w_group([expect])[0]

        x_sem = [S.dma_sem("xld%d" % b) for b in range(2)]
        rot_sem = S.dma_sem("rotld")
        out_sem = [S.dma_sem("ost%d" % b) for b in range(2)]

        def load_x(t):
            b = t % 2
            ops = []
            for s in range(4):
                ops.append(dma("pool", hbuf[b][:, s, :], x_d[t * 512 + s * 128:t * 512 + (s + 1) * 128, :], [], [hres[b][s]], x_sem[b]))
            for o in ops:
                o.dval = x_sem[b].count

        def load_rot(t):
            dma("pool", rot[:], rot_d[t * 512:(t + 1) * 512, :].rearrange("(s p) c -> p s c", p=P), [], [rot_res], rot_sem)

        def rms_stats(b, reads_extra=()):
            sa, sr = new_stat()
            S.add("dve", lambda e: e.memset(sa[:, 0:4], 0.0), [], [sr])
            for s in range(4):
                act(junk[:], hbuf[b][:, s, :], AF.Square, [hres[b][s]], [sr], n=1024, accum_out=sa[:, s:s + 1])
            ts(sa[:, 4:8], sa[:, 0:4], 1.0 / 1024, EPS, ALU.mult, ALU.add, [sr], [sr])
            act(sa[:, 8:12], sa[:, 4:8], AF.Sqrt, [sr], [sr], n=4)
            S.add("dve", lambda e: e.reciprocal(out=sa[:, 12:16], in_=sa[:, 8:12]), [sr], [sr])
            return sa, sr

        def rmsnorm_T(b):
            sa, sr = rms_stats(b)
            for s in range(4):
                k = s % 2
                act(hn_tm[k][:], hbuf[b][:, s, :], AF.Copy, [hres[b][s], sr], [hn_tm_res[k]], n=1024, scale=sa[:, 12 + s:13 + s])
                pc, pr = palloc(4)
                pv = ps[:, pc:pc + 512].bitcast(BF16)
                for kc in range(8):
                    trp(pv[:, kc * 128:(kc + 1) * 128], hn_tm[k][:, kc * 128:(kc + 1) * 128], [hn_tm_res[k]], pr)
                eng = pick(1024)
                copy(eng, hnT[:, :, s * 128:(s + 1) * 128], pv.rearrange("p (a b) -> p a b", a=8), pr, [hnT_res[s]], n=1024)

        def page_tokmajor(pi, evac, kcn=8, lhs=None, lhs_res=None, acc_first=True):
            wv, wr = w_next(pi)
            for s in range(4):
                pc, pr = palloc(4)
                for kc in range(kcn):
                    mm(ps[:, pc:pc + 512], hnT[:, kc, s * 128:(s + 1) * 128], wv[:, kc, :], kc == 0, kc == kcn - 1,
                       [hnT_res[s], wr], pr)
                evac(s, pc, pr)

        def page_featmajor(pi, evac):
            wv, wr = w_next(pi)
            for oc in range(4):
                pc, pr = palloc(4)
                for kc in range(8):
                    mm(ps[:, pc:pc + 512], wv[:, kc, oc * 128:(oc + 1) * 128], hnT[:, kc, :], kc == 0, kc == 7,
                       list(hnT_res) + [wr], pr)
                evac(oc, pc, pr)

        def proj_residual(pages, kgroups, lhs_of, lhs_res_of, b):
            for cg in range(2):
                wvs = w_group(pages[cg])
                nk = sum(kgroups)
                for s in range(4):
                    pc, pr = palloc(4)
                    kk = 0
                    for (wv, wr), kn in zip(wvs, kgroups):
                        for kc in range(kn):
                            mm(ps[:, pc:pc + 512], lhs_of(kk, s), wv[:, kc, :], kk == 0, kk == nk - 1,
                               list(lhs_res_of(kk, s)) + [wr], pr)
                            kk += 1
                    tt(hbuf[b][:, s, cg * 512:(cg + 1) * 512], hbuf[b][:, s, cg * 512:(cg + 1) * 512], ps[:, pc:pc + 512],
                       ALU.add, pr + [hres[b][s]], [hres[b][s]])

        def rotary_evac(pc, pr, s, dst, dst_res, rt, rt_res):
            z = ps[:, pc:pc + 512].rearrange("p (h d) -> p h d", h=4)
            A = rt[0].rearrange("p (h d) -> p h d", h=4)
            B = rt[1].rearrange("p (h d) -> p h d", h=4)
            cc = rot[:, s, 0:128].unsqueeze(1).broadcast_to([P, 4, 128])
            sn_lo = rot[:, s, 128:192].unsqueeze(1).broadcast_to([P, 4, 64])
            sn_hi = rot[:, s, 192:256].unsqueeze(1).broadcast_to([P, 4, 64])
            tt(A, z, cc, ALU.mult, pr + [rot_res], [rt_res[0]])
            tt(B[:, :, 0:64], z[:, :, 64:128], sn_lo, ALU.mult, pr + [rot_res], [rt_res[1]], n=256)
            tt(B[:, :, 64:128], z[:, :, 0:64], sn_hi, ALU.mult, pr + [rot_res], [rt_res[1]], n=256)
            tt(dst, rt[0], rt[1], ALU.add, [rt_res[0], rt_res[1]], [dst_res])

        def l0_inproj(t, kind, bufs):
            half = t % 2
            full = kind != "warm"
            last_warm = (t == n_warm - 1)
            L = bufs

            def ev_q(s, pc, pr):
                k = s % 2
                rotary_evac(pc, pr, s, L["qa_tm"][:, k, :], L["qa_tm_r"][k], L["rtmp"], L["rtmp_r"])
                qc, qr = palloc(2)
                qv = ps[:, qc:qc + 256].bitcast(BF16)
                for h in range(4):
                    trp(qv[:, h * 128:(h + 1) * 128], L["qa_tm"][:, k, h * 128:(h + 1) * 128], [L["qa_tm_r"][k]], qr)
                copy(pick(512), L["qaT"][:, :, s * 128:(s + 1) * 128], qv.rearrange("p (h d) -> p h d", h=4), qr, [L["qaT_r"][s]])

            def ev_k(s, pc, pr):
                k = s % 2
                rotary_evac(pc, pr, s, L["ka_tm"][:, k, :], L["ka_tm_r"][k], L["rtmp"], L["rtmp_r"])
                if full:
                    qc, qr = palloc(2)
                    qv = ps[:, qc:qc + 256].bitcast(BF16)
                    for h in range(4):
                        trp(qv[:, h * 128:(h + 1) * 128], L["ka_tm"][:, k, h * 128:(h + 1) * 128], [L["ka_tm_r"][k]], qr)
                    copy(pick(512), L["kaT"][:, :, s * 128:(s + 1) * 128], qv.rearrange("p (h d) -> p h d", h=4), qr, [L["kaT_r"][s]])
                o, _ = COLS["kdec"]
                tt(L["kdec"][:, s, :].rearrange("p (h d) -> p h d", h=4), L["ka_tm"][:, k, :].rearrange("p (h d) -> p h d", h=4),
                   cols[:, o:o + 4].unsqueeze(2).broadcast_to([P, 4, 128]), ALU.mult, [L["ka_tm_r"][k], const_res], [L["kdec_r"][s]])

            def ev_v(pg):
                def f(s, pc, pr):
                    copy(pick(512), L["va"][:, s, pg * 512:(pg + 1) * 512], ps[:, pc:pc + 512], pr, [L["va_r"][s][pg]])
                return f

            def ev_g(pg):
                def f(s, pc, pr):
                    act(L["sg"][:, s, pg * 512:(pg + 1) * 512], ps[:, pc:pc + 512], AF.Silu, pr, [L["sg_r"][s][pg]])
                return f

            def ev_qb(oc, pc, pr):
                copy(pick(512), L["qbT"][:, oc, :], ps[:, pc:pc + 512], pr, [L["qbT_r"][oc]])

            def ev_kb(oc, pc, pr):
                copy(pick(512), kT[:, oc, half * 512:(half + 1) * 512], ps[:, pc:pc + 512], pr, [kT_res[half][oc]])

            def ev_vb(s, pc, pr):
                dst = Vr[:, half * 4 + s, :, 0:64]
                src = ps[:, pc:pc + 512].rearrange("p (h d) -> p h d", h=8)
                if kind == "main":
                    copy(pick(512), dst, src, pr, [V_res[half][s]])
                else:
                    act(dst, src, AF.Copy, pr + [const_res], [V_res[half][s]], scale=colap("flag"))

            pin = PIDX["l0_in"]
            if full:
                page_tokmajor(pin[0], ev_q)
            page_tokmajor(pin[1], ev_k)
            page_tokmajor(pin[2], ev_v(0))
            page_tokmajor(pin[3], ev_v(1))
            if full:
                page_tokmajor(pin[4], ev_g(0))
                page_tokmajor(pin[5], ev_g(1))
                page_featmajor(pin[6], ev_qb)
            if full or last_warm:
                src = colap("one") if kind == "main" else colap("flag")
                S.add("dve", lambda e: e.tensor_copy(out=Vr[:, half * 4:half * 4 + 4, :, 64:65],
                                                     in_=src.unsqueeze(1).unsqueeze(1).broadcast_to([P, 4, 8, 1])),
                      [const_res], V_res[half])
                page_featmajor(pin[7], ev_kb)
                page_tokmajor(pin[8], ev_vb)

        def retention_state(s, L, need_bf, nxt):
            for h in range(4):
                pc, pr = palloc(2)
                mm(ps[:, pc:pc + 256], L["kdec"][:, s, h * 128:(h + 1) * 128], L["va"][:, s, h * 256:(h + 1) * 256], True, True,
                   [L["kdec_r"][s], L["va_r"][s][h // 2]], pr)
                stt(Sst[:, h, :], Sst[:, h, :], G128[h], ps[:, pc:pc + 256], ALU.mult, ALU.add, pr + [S_res[h]], [S_res[h]], n=256)
                if need_bf:
                    copy("act", L["S_bf"][:, nxt, h, :], Sst[:, h, :], [S_res[h]], [L["S_bf_r"][nxt][h]], n=256)

        def l0_mixer(t, kind, b):
            half = t % 2
            L = {}
            for name, dt, shp in (("va", BF16, "p (s c) -> p s c"), ("sg", BF16, "p (s c) -> p s c"), ("kdec", BF16, "p (s c) -> p s c"),
                                  ("qaT", BF16, "p (h c) -> p h c"), ("kaT", BF16, "p (h c) -> p h c"), ("qbT", BF16, "p (h c) -> p h c"),
                                  ("qa_tm", BF16, "p (s c) -> p s c"), ("ka_tm", BF16, "p (s c) -> p s c")):
                ap, r = carve(L0_LAY, name, dt, None)
                n1 = {"va": 4, "sg": 4, "kdec": 4, "qaT": 4, "kaT": 4, "qbT": 4, "qa_tm": 2, "ka_tm": 2}[name]
                L[name] = ap.rearrange(shp, **{shp.split("(")[1][0]: n1})
                L[name + "_r"] = split_res(r, n1)
            for name in ("va", "sg"):
                L[name + "_r"] = [split_res(r, 2) for r in L[name + "_r"]]
            ap, r = carve(L0_LAY, "S_bf", BF16, None)
            L["S_bf"] = ap.rearrange("p (k h e) -> p k h e", k=2, h=4)
            L["S_bf_r"] = [split_res(r, 4) for _ in range(2)]
            ap, r = carve(L0_LAY, "rtmp", F32, None)
            L["rtmp"] = [ap[:, 0:512], ap[:, 512:1024]]
            L["rtmp_r"] = split_res(r, 2)
            if kind == "warm":
                l0_inproj(t, kind, L)
                for s in range(4):
                    retention_state(s, L, False, 0)
                return
            ap, r = carve(L0_LAY, "y_tm", BF16, None)
            y_tm = ap.rearrange("p (k c) -> p k c", k=2)
            y_tm_r = [split_res(r, 6) for _ in range(2)]
            ap, r = carve(L0_LAY, "yT", BF16, None)
            yT = ap.rearrange("p (k c) -> p k c", k=12)
            yT_r = [split_res(r, 3) for _ in range(4)]
            ap, r = carve(L0_LAY, "p32", F32, None)
            p32 = ap.rearrange("p (k c) -> p k c", k=2)
            p32_r = split_res(r, 2)
            ap, r = carve(L0_LAY, "pb", BF16, None)
            pb = ap.rearrange("p (k c) -> p k c", k=2)
            pb_r = split_res(r, 2)
            ap, r = carve(L0_LAY, "o_sb", F32, None)
            o_sb = ap.rearrange("p (k c) -> p k c", k=2)
            o_sb_r = split_res(r, 2)
            ap, r = carve(L0_LAY, "t_sb", F32, None)
            t_sb = ap.rearrange("p (k c) -> p k c", k=2)
            t_sb_r = split_res(r, 2)
            ap, r = carve(L0_LAY, "sTm", BF16, None)
            sTm = ap.rearrange("p (k c) -> p k c", k=2)
            sTm_r = split_res(r, 2)
            for h in range(4):
                copy("act", L["S_bf"][:, 0, h, :], Sst[:, h, :], [S_res[h]], [L["S_bf_r"][0][h]], n=256)
            rmsnorm_T(b)
            cut(3)
            l0_inproj(t, kind, L)
            cut(4)
            if t + 1 < NT:
                load_rot(t + 1)
            qd, _ = COLS["qdec"]
            cnt = 0
            for s in range(4):
                cur, nxt = s % 2, (s + 1) % 2
                yk = s % 2
                sl = slice(s * 128, (s + 1) * 128)
                pc, pr = palloc()
                for h in range(4):
                    mm(ps[:, pc + h * 128:pc + (h + 1) * 128], L["kaT"][:, h, sl], L["qaT"][:, h, sl], True, True,
                       [L["kaT_r"][s], L["qaT_r"][s]], pr)
                tt(sTm[:, cur, :].rearrange("p (h i) -> p h i", h=4), ps[:, pc:pc + 512].rearrange("p (h i) -> p h i", h=4), Dq[:],
                   ALU.mult, pr + [const_res], [sTm_r[cur]])
                for hp2 in range(2):
                    oc_, or_ = palloc()
                    for hh in range(2):
                        h = hp2 * 2 + hh
                        mm(ps[:, oc_ + hh * 256:oc_ + (hh + 1) * 256], sTm[:, cur, h * 128:(h + 1) * 128], L["va"][:, s, h * 256:(h + 1) * 256], True, False,
                           [sTm_r[cur], L["va_r"][s][h // 2]], or_)
                        mm(ps[:, oc_ + hh * 256:oc_ + (hh + 1) * 256], L["qaT"][:, h, sl], L["S_bf"][:, cur, h, :], False, True,
                           [L["qaT_r"][s], L["S_bf_r"][cur][h]], or_)
                    for hh in range(2):
                        h = hp2 * 2 + hh
                        k2 = cnt % 2
                        cnt += 1
                        act(o_sb[:, k2, :], ps[:, oc_ + hh * 256:oc_ + (hh + 1) * 256], AF.Copy, or_ + [const_res], [o_sb_r[k2]], n=256,
                            scale=cols[:, qd + h:qd + h + 1])
                        sa, sr = new_stat()
                        S.add("dve", lambda e, sa=sa, k2=k2: e.bn_stats(out=sa[:, 0:6], in_=o_sb[:, k2, :]), [o_sb_r[k2]], [sr])
                        S.add("dve", lambda e, sa=sa: e.bn_aggr(out=sa[:, 6:8], in_=sa[:, 0:6]), [sr], [sr])
                        ts(sa[:, 8:9], sa[:, 7:8], EPS, None, ALU.add, None, [sr], [sr])
                        act(sa[:, 9:10], sa[:, 8:9], AF.Sqrt, [sr], [sr], n=1)
                        S.add("dve", lambda e, sa=sa: e.reciprocal(out=sa[:, 10:11], in_=sa[:, 9:10]), [sr], [sr])
                        stt(t_sb[:, k2, :], o_sb[:, k2, :], sa[:, 6:7], L["sg"][:, s, h * 256:(h + 1) * 256], ALU.subtract, ALU.mult,
                            [o_sb_r[k2], sr, L["sg_r"][s][h // 2]], [t_sb_r[k2]], n=256)
                        act(y_tm[:, yk, h * 256:(h + 1) * 256], t_sb[:, k2, :], AF.Copy, [t_sb_r[k2], sr], [y_tm_r[yk][h]], n=256, scale=sa[:, 10:11])
                for hp2 in range(2):
                    kc_, kr_ = palloc()
                    for hh in range(2):
                        h = hp2 * 2 + hh
                        mm(ps[:, kc_ + hh * 256:kc_ + (hh + 1) * 256], L["kdec"][:, s, h * 128:(h + 1) * 128], L["va"][:, s, h * 256:(h + 1) * 256], True, True,
                           [L["kdec_r"][s], L["va_r"][s][h // 2]], kr_)
                    for hh in range(2):
                        h = hp2 * 2 + hh
                        stt(Sst[:, h, :], Sst[:, h, :], G128[h], ps[:, kc_ + hh * 256:kc_ + (hh + 1) * 256], ALU.mult, ALU.add, kr_ + [S_res[h]], [S_res[h]], n=256)
                        copy("act", L["S_bf"][:, nxt, h, :], Sst[:, h, :], [S_res[h]], [L["S_bf_r"][nxt][h]], n=256)
                cut(5)
                blocks = [(1 - half, j) for j in range(s, 4)] + [(half, j) for j in range(0, s + 1)]
                gbank = {}
                zprev = [None]

                def emit_scores(q):
                    cX, rX = palloc()
                    cY, rY = palloc()
                    cZ, rZ = palloc()
                    info = []
                    for idx, (cb, rb) in enumerate(((cX, rX), (cY, rY))):
                        h = 2 * q + idx
                        po = idx * 64
                        for kb, (hf, j) in enumerate(blocks):
                            dst = ps[:, cb + kb * 128:cb + (kb + 1) * 128] if kb < 4 else ps[:, cZ + idx * 128:cZ + (idx + 1) * 128]
                            fz = [zprev[0]] if (kb == 4 and idx == 1) else []
                            o_ = mm(dst, kT[po:po + 64, q, hf * 512 + j * 128:hf * 512 + (j + 1) * 128], L["qbT"][po:po + 64, q, sl],
                                    True, True, [kT_res[hf][q], L["qbT_r"][q]], rb if kb < 4 else rZ, force=fz)
                            if kb == 4 and idx == 0:
                                zprev[0] = o_
                    cut(61)
                    for idx, (cb, rb) in enumerate(((cX, rX), (cY, rY))):
                        h = 2 * q + idx
                        k2 = (2 * q + idx) % 2
                        act(p32[:, k2, 0:512], ps[:, cb:cb + 512], AF.Exp, rb, [p32_r[k2]], n=512, scale=0.125)
                        act(p32[:, k2, 512:640], ps[:, cZ + idx * 128:cZ + (idx + 1) * 128], AF.Exp, rZ, [p32_r[k2]], n=128, scale=0.125)
                        cut(62)
                        tt(pb[:, k2, :], p32[:, k2, :], EB[:, h, :], ALU.mult, [p32_r[k2], setup_res], [pb_r[k2]], n=640)
                        cut(63)
                        info.append((h, k2))
                    return info

                def emit_pv(q, info):
                    g = q // 2
                    if q % 2 == 0:
                        gbank[g] = palloc()
                    gc, gr = gbank[g]
                    for (h, k2) in info:
                        hh = h % 4
                        for kb, (hf, j) in enumerate(blocks):
                            mm(ps[:, gc + hh * 65:gc + hh * 65 + 65], pb[:, k2, kb * 128:(kb + 1) * 128], Vr[:, hf * 4 + j, h, :], kb == 0, kb == 4,
                               [pb_r[k2], V_res[hf][j]], gr)
                    cut(64)
                    if q % 2 == 1:
                        cut(67)
                        sa, sr = new_stat()
                        ov = ps[:, gc:gc + 260].rearrange("p (h d) -> p h d", h=4)
                        ts(sa[:, 0:4].unsqueeze(2), ov[:, :, 64:65], 1e-30, None, ALU.add, None, gr, [sr])
                        cut(68)
                        S.add("dve", lambda e, sa=sa: e.reciprocal(out=sa[:, 4:8], in_=sa[:, 0:4]), [sr], [sr])
                        cut(65)
                        tt(y_tm[:, yk, 1024 + g * 256:1024 + (g + 1) * 256].rearrange("p (h d) -> p h d", h=4), ov[:, :, 0:64],
                           sa[:, 4:8].unsqueeze(2).broadcast_to([P, 4, 64]), ALU.mult, gr + [sr], [y_tm_r[yk][4 + g]], n=256)
                        cut(66)

                for q in range(4):
                    info = emit_scores(q)
                    emit_pv(q, info)
                cut(6)
                for (k0, kn) in ((0, 8), (8, 4)):
                    qc, qr = palloc()
                    qv = ps[:, qc:qc + 512].bitcast(BF16)
                    rr = []
                    for i in range(kn):
                        kc = k0 + i
                        r1 = y_tm_r[yk][kc // 2] if kc < 8 else y_tm_r[yk][4 + (kc - 8) // 2]
                        trp(qv[:, i * 128:(i + 1) * 128], y_tm[:, yk, kc * 128:(kc + 1) * 128], [r1], qr)
                    dst_r = [yT_r[s][0], yT_r[s][1]] if k0 == 0 else [yT_r[s][2]]
                    copy(pick(512), yT[:, k0:k0 + kn, sl], qv[:, 0:kn * 128].rearrange("p (a c) -> p a c", a=kn), qr, dst_r, n=kn * 128)
            cut(7)
            po_ = PIDX["l0_out"]
            proj_residual([[po_[0], po_[1]], [po_[2], po_[3]]], (8, 4),
                          lambda kk, s: yT[:, kk, s * 128:(s + 1) * 128], lambda kk, s: [yT_r[s][kk // 4]], b)

        def conv_ffn(l, b):
            ap, r = carve(FFN_LAY, "aT", BF16, None)
            aT = ap.rearrange("p (c t) -> p c t", c=22)
            aT_r = split_res(r, 22)
            ap, r = carve(FFN_LAY, "acc", F32, None)
            acc = ap.rearrange("p (k t) -> p k t", k=4)
            acc_r = split_res(r, 4)
            rmsnorm_T(b)
            cw, _ = COLS["cw%d" % l]
            cb, _ = COLS["cb%d" % l]
            pair = [0]

            def conv_evac(cidx, pc, pr, k):
                z = ps[:, pc:pc + 512]
                w0 = cols[:, cw + cidx * 3 + 0:cw + cidx * 3 + 1]
                w1 = cols[:, cw + cidx * 3 + 1:cw + cidx * 3 + 2]
                w2 = cols[:, cw + cidx * 3 + 2:cw + cidx * 3 + 3]
                bb = cols[:, cb + cidx:cb + cidx + 1]
                zr = zc_res[l][cidx]
                act(acc[:, k, :], z, AF.Identity, pr + [const_res], [acc_r[k]], scale=w2, bias=bb)
                stt(acc[:, k, 1:512], z[:, 0:511], w1, acc[:, k, 1:512], ALU.mult, ALU.add, pr + [acc_r[k]], [acc_r[k]])
                stt(acc[:, k, 2:512], z[:, 0:510], w0, acc[:, k, 2:512], ALU.mult, ALU.add, pr + [acc_r[k]], [acc_r[k]])
                stt(acc[:, k, 0:1], zc[l][:, cidx, 1:2], w1, acc[:, k, 0:1], ALU.mult, ALU.add, [zr, acc_r[k]], [acc_r[k]], n=1)
                stt(acc[:, k, 0:2], zc[l][:, cidx, 0:2], w0, acc[:, k, 0:2], ALU.mult, ALU.add, [zr, acc_r[k]], [acc_r[k]], n=2)
                copy("act", zc[l][:, cidx, :], z[:, 510:512], pr, [zr], n=2)

            def ev_up(pgk):
                def f(oc, pc, pr):
                    if oc < 2:
                        c = 2 * pgk + oc
                        k = oc
                        conv_evac(c, pc, pr, k)
                        act(acc[:, k, :], acc[:, k, :], AF.Gelu_apprx_tanh, [acc_r[k]], [acc_r[k]])
                    else:
                        c = 2 * pgk + (oc - 2)
                        k = oc
                        conv_evac(22 + c, pc, pr, k)
                        tt(aT[:, c, :], acc[:, oc - 2, :], acc[:, k, :], ALU.mult, [acc_r[oc - 2], acc_r[k]], [aT_r[c]])
                return f

            for k, pi in enumerate(PIDX["f%d_up" % l]):
                page_featmajor(pi, ev_up(k))
            pd = PIDX["f%d_dn" % l]
            proj_residual([pd[0:3], pd[3:6]], (8, 8, 6), lambda kk, s: aT[:, kk, s * 128:(s + 1) * 128], lambda kk, s: [aT_r[kk]], b)

        def sgu_mixer(b):
            ap, r = carve(SGU_LAY, "uT", BF16, None)
            uT = ap.rearrange("p (c t) -> p c t", c=16)
            uT_r = split_res(r, 16)
            ap, r = carve(SGU_LAY, "vg", BF16, None)
            vg = ap.rearrange("p (s c) -> p s c", s=4)
            vg_r = split_res(r, 4)
            ap, r = carve(SGU_LAY, "yT2", BF16, None)
            yT2 = ap.rearrange("p (c t) -> p c t", c=16)
            yT2_r = [split_res(r, 4) for _ in range(4)]
            ap, r = carve(SGU_LAY, "m4", F32, None)
            m4 = ap.rearrange("p (k t) -> p k t", k=2)
            m4_r = split_res(r, 2)
            rmsnorm_T(b)
            pin = PIDX["l1_in"]

            def ev_u(pg):
                def f(oc, pc, pr):
                    act(uT[:, pg * 4 + oc, :], ps[:, pc:pc + 512], AF.Gelu_apprx_tanh, pr, [uT_r[pg * 4 + oc]])
                return f

            def ev_v(pg):
                def f(s, pc, pr):
                    act(vg[:, s, pg * 512:(pg + 1) * 512], ps[:, pc:pc + 512], AF.Gelu_apprx_tanh, pr, [vg_r[s]])
                return f

            for pg in range(4):
                page_featmajor(pin[pg], ev_u(pg))
            for pg in range(4):
                page_tokmajor(pin[4 + pg], ev_v(pg))
            lg, _ = COLS["ln_g"]
            cnt = 0
            for s in range(4):
                sa, sr = new_stat()
                st24 = stat24[s % 2]
                st24_r = stat24_res[s % 2]
                for q in range(4):
                    S.add("dve", lambda e, q=q, s=s, st24=st24: e.bn_stats(out=st24[:, q * 6:(q + 1) * 6], in_=vg[:, s, q * 512:(q + 1) * 512]), [vg_r[s]], [st24_r])
                S.add("dve", lambda e, sa=sa, st24=st24: e.bn_aggr(out=sa[:, 0:2], in_=st24[:].rearrange("p (q d) -> p q d", q=4)), [st24_r], [sr])
                ts(sa[:, 2:3], sa[:, 1:2], EPS, None, ALU.add, None, [sr], [sr])
                act(sa[:, 3:4], sa[:, 2:3], AF.Sqrt, [sr], [sr], n=1)
                S.add("dve", lambda e, sa=sa: e.reciprocal(out=sa[:, 4:5], in_=sa[:, 3:4]), [sr], [sr])
                ts(vg[:, s, :], vg[:, s, :], sa[:, 0:1], sa[:, 4:5], ALU.subtract, ALU.mult, [vg_r[s], sr], [vg_r[s]], n=2048)
                for c4 in range(4):
                    k2 = cnt % 2
                    cnt += 1
                    pc, pr = palloc(4)
                    for i in range(4):
                        c = c4 * 4 + i
                        mm(ps[:, pc + i * 128:pc + (i + 1) * 128], vg[:, s, c * 128:(c + 1) * 128], wTm[:, c // 2, :], True, True,
                           [vg_r[s], setup_res], pr)
                    for i in range(4):
                        c = c4 * 4 + i
                        stt(m4[:, k2, i * 128:(i + 1) * 128], ps[:, pc + i * 128:pc + (i + 1) * 128], cols[:, lg + c:lg + c + 1], Rt[:, c, :],
                            ALU.mult, ALU.add, pr + [setup_res, const_res], [m4_r[k2]], n=128)
                    tt(yT2[:, c4 * 4:(c4 + 1) * 4, s * 128:(s + 1) * 128], m4[:, k2, :].rearrange("p (a t) -> p a t", a=4),
                       uT[:, c4 * 4:(c4 + 1) * 4, s * 128:(s + 1) * 128], ALU.mult, [m4_r[k2]] + uT_r[c4 * 4:(c4 + 1) * 4], [yT2_r[s][c4]])
            po_ = PIDX["l1_out"]
            proj_residual([[po_[0], po_[1]], [po_[2], po_[3]]], (8, 8),
                          lambda kk, s: yT2[:, kk, s * 128:(s + 1) * 128], lambda kk, s: [yT2_r[s][kk // 4]], b)

        stat24 = [sb("stat24_%d" % i, (P, 24), F32) for i in range(2)]
        stat24_res = [Res("stat24_%d" % i) for i in range(2)]

        load_x(0)
        load_rot(0)
        if CUT < 3:
            for s_ in range(4):
                dma("pool", out_d[s_ * 128:(s_ + 1) * 128, :], hbuf[0][:, s_, :], [hres[0][s_]], [], out_sem[0])
        else:
            w_prefetch(NSLOT - 1)
        try:
            for t in range(NT if CUT >= 3 else 0):
                S.epoch = t + 1
                b = t % 2
                kind = "warm" if t < n_warm else ("halo" if t == n_warm else "main")
                if t + 1 < NT:
                    load_x(t + 1)
                if kind == "warm":
                    rmsnorm_T(b)
                    l0_mixer(t, kind, b)
                    load_rot(t + 1)
                    continue
                l0_mixer(t, kind, b)
                if stop_stage >= 2:
                    conv_ffn(0, b)
                if kind == "halo":
                    for s in range(4):
                        ts(hbuf[b][:, s, :], hbuf[b][:, s, :], colap("flag"), None, ALU.mult, None, [hres[b][s], const_res], [hres[b][s]], n=1024)
                    if stop_stage < 3:
                        continue
                if stop_stage >= 3:
                    sgu_mixer(b)
                if stop_stage >= 4:
                    conv_ffn(1, b)
                if kind == "halo":
                    continue
                if stop_stage >= 5:
                    sa, sr = rms_stats(b)
                    for s in range(4):
                        stt(hbuf[b][:, s, :], hbuf[b][:, s, :], sa[:, 12 + s:13 + s], gfin[:], ALU.mult, ALU.mult,
                            [hres[b][s], sr, const_res], [hres[b][s]], n=1024)
                mt = t - n_warm - 1
                oops = []
                for s in range(4):
                    oops.append(dma("pool", out_d[mt * 512 + s * 128:mt * 512 + (s + 1) * 128, :], hbuf[b][:, s, :], [hres[b][s]], [], out_sem[b]))
                for o in oops:
                    o.dval = out_sem[b].count

        except _Stop:
            for s_ in range(4):
                dma("pool", out_d[s_ * 128:(s_ + 1) * 128, :], hbuf[0][:, s_, :], [hres[0][s_]], [], out_sem[0])
        assert CUT < 99 or wstate["used"] == len(seq), (wstate, len(seq))
        S.emit(final_waits=out_sem)
        build_program.stats = dict(S.stats)
    return nc


def _const_tables():
    f64 = np.float64
    gam = np.array([1.0 - 2.0 ** (-5 - h) for h in range(4)], f64)
    i = np.arange(128)
    ci = i // 64
    scale = 128.0 ** -0.5
    dq = np.zeros((128, 4, 128), f64)
    for h in range(4):
        g = gam[h]
        jj, ii = np.meshgrid(i, i, indexing="ij")
        same = (jj // 64) == (ii // 64)
        earlier = (jj // 64) < (ii // 64)
        Dm = np.where(same, g ** np.abs(ii - jj), np.where(earlier, g ** np.maximum(ii - jj, 0), 0.0))
        dq[:, h, :] = scale * Dm / (g ** (ii + 1.0))
    kdec = np.stack([scale * gam[h] ** (127.0 - i) for h in range(4)], axis=1)
    qdec = np.stack([gam[h] ** (i + 1.0) for h in range(4)], axis=1)
    return dq.astype(np.float32), kdec.astype(np.float32), qdec.astype(np.float32)


def _rel_index():
    i = np.arange(128)[None, :]
    j = np.arange(640)[:, None]
    ci = i // 64
    jc = j // 64
    inband = (jc >= ci) & (jc <= ci + 8)
    qpos = (i % 64) + 512
    kpos = j - 64 * ci
    rel = np.clip(qpos - kpos, -128, 128) + 128
    return rel, inband


def _rot_table(pos):
    f32 = np.float32
    lin = np.linspace(0.0, 1.0, 64, dtype=f32)
    inv = (f32(1.0) / np.power(f32(10000.0), lin).astype(f32)).astype(f32)
    ang = (pos.astype(f32)[:, None] * inv[None, :]).astype(f32).astype(np.float64)
    c = np.cos(ang)
    s = np.sin(ang)
    return np.concatenate([c, c, -s, s], axis=1).astype(f32)


def _shared_inputs(inp):
    f32 = np.float32
    sh = {}
    sh["w_in"] = np.ascontiguousarray(inp["ab_w_in"][0], f32)
    sh["w_out"] = np.ascontiguousarray(inp["ab_w_out"][0], f32)
    sh["w_up0"] = np.ascontiguousarray(inp["ffn_w_up"][0], f32)
    sh["w_up1"] = np.ascontiguousarray(inp["ffn_w_up"][1], f32)
    sh["w_dn0"] = np.ascontiguousarray(inp["ffn_w_down"][0], f32)
    sh["w_dn1"] = np.ascontiguousarray(inp["ffn_w_down"][1], f32)
    sh["c_in"] = np.ascontiguousarray(inp["c_w_in"][0], f32)
    sh["c_out"] = np.ascontiguousarray(inp["c_w_out"][0], f32)
    dq, kdec, qdec = _const_tables()
    cols = np.zeros((P, NCOLS), f32)

    def put(name, arr):
        o, n = COLS[name]
        cols[:, o:o + n] = np.asarray(arr, f32).reshape(P, n)

    colmajor = lambda v: np.asarray(v, f32).reshape(-1, P).T
    put("g_attn0", colmajor(inp["attn_norm_g"][0]))
    put("g_ffn0", colmajor(inp["ffn_norm_g"][0]))
    put("g_attn1", colmajor(inp["attn_norm_g"][1]))
    put("g_ffn1", colmajor(inp["ffn_norm_g"][1]))
    for l in range(2):
        cw = np.asarray(inp["ffn_conv_w"][l], f32)
        put("cw%d" % l, cw.T.reshape(44, P, 3).transpose(1, 0, 2).reshape(P, 44 * 3))
        put("cb%d" % l, colmajor(inp["ffn_conv_b"][l]))
    put("ln_g", colmajor(inp["c_ln_g"][0]))
    put("ln_b", colmajor(inp["c_ln_b"][0]))
    put("kdec", kdec)
    put("qdec", qdec)
    put("one", np.ones((P, 1), f32))
    sh["cols"] = cols
    sh["gfin"] = np.ascontiguousarray(np.broadcast_to(np.asarray(inp["final_norm_g"], f32)[None, :], (P, 1024)))
    sh["ident"] = np.eye(P, dtype=f32)
    rel, inband = _rel_index()
    rb = np.asarray(inp["ab_rel_bias"][0], f32)
    B = np.where(inband[None], rb[:, rel], f32(-1e30)).astype(f32)
    sh["biasT"] = np.ascontiguousarray(B.reshape(8, 5, 128, 128).transpose(2, 0, 1, 3).reshape(P, 8 * 5 * 128))
    sh["dq"] = np.ascontiguousarray(dq.reshape(P, 4 * 128))
    ws = np.asarray(inp["c_w_s"][0], f32)
    sh["wsT"] = np.ascontiguousarray(ws.transpose(2, 0, 1).reshape(P, 8 * 128))
    ii = np.arange(128)
    sh["maskT"] = ((ii[:, None] // 64) <= (ii[None, :] // 64)).astype(f32)
    bs = np.asarray(inp["c_b_s"][0], f32)
    sh["bsb"] = np.ascontiguousarray(np.broadcast_to(bs.reshape(1, 8 * 128), (P, 8 * 128)))
    return sh


def run_module(inp, seg, n_warm, stop_stage=5):
    x = np.asarray(inp["x"], np.float32)
    Bn, T, Dm = x.shape
    nseg = T // seg
    n_cores = Bn * nseg
    n_main = seg // 512
    NT = n_warm + 1 + n_main
    nc = build_program(n_warm, n_main, stop_stage)
    sh = _shared_inputs(inp)
    in_maps = []
    for c in range(n_cores):
        bi, si = divmod(c, nseg)
        p0 = si * seg
        start = p0 - 512 * (n_warm + 1)
        xs = np.zeros((NT * 512, Dm), np.float32)
        lo = max(start, 0)
        xs[lo - start:] = x[bi, lo:p0 + seg]
        m = dict(sh)
        m["x"] = xs
        m["rot"] = _rot_table(np.maximum(np.arange(start, p0 + seg), 0))
        cols = sh["cols"].copy()
        o, _ = COLS["flag"]
        cols[:, o] = 0.0 if si == 0 else 1.0
        m["cols"] = cols
        in_maps.append(m)
    res = run_bass_kernel_spmd(nc, in_maps, core_ids=list(range(n_cores)))
    out = np.zeros((Bn, T, Dm), np.float32)
    for c in range(n_cores):
        bi, si = divmod(c, nseg)
        out[bi, si * seg:(si + 1) * seg] = res.results[c]["out"]
    return out


def kernel(**inputs):
    return run_module(inputs, seg=4096, n_warm=8, stop_stage=5)
```

```python
import numpy as np
from contextlib import ExitStack
import concourse.bass as bass
import concourse.mybir as mybir
from concourse.bass_utils import run_bass_kernel_spmd

F32, BF16 = mybir.dt.float32, mybir.dt.bfloat16
AF = mybir.ActivationFunctionType
ALU = mybir.AluOpType

import os
CUT = int(os.environ.get('K_CUT', '99'))
P = 128
EPS = 1e-6
NSLOT = 4
ENGS = ("pe", "act", "dve", "pool", "sp")


def _okey(op):
    return op.dsem.key if op.dsem is not None else op.eng


class Res:
    __slots__ = ("name", "w", "r", "kids", "excl")

    def __init__(self, name="", inherit=()):
        self.name = name
        self.excl = False
        self.w = None
        self.r = {}
        self.kids = []
        stack = list(inherit)
        while stack:
            o = stack.pop()
            if o.w is not None:
                self._put(o.w)
            for q in o.r.values():
                self._put(q)
            stack.extend(o.kids)

    def _put(self, op):
        k = _okey(op)
        cur = self.r.get(k)
        if cur is None or cur.idx < op.idx:
            self.r[k] = op


class DmaSem:
    def __init__(self, sem, key):
        self.sem = sem
        self.key = key
        self.count = 0


import os


class _Stop(Exception):
    pass


CUTK = int(os.environ.get('K_CUTK', '1'))
_hits = {}


def cut(n):
    if CUT == n:
        _hits[n] = _hits.get(n, 0) + 1
        if _hits[n] >= CUTK:
            raise _Stop()


class Op:
    __slots__ = ("eng", "fn", "deps", "sig", "dsem", "dval", "waits", "clock", "need", "epoch", "key", "idx")


class Sched:
    def __init__(self, nc, stack):
        self.nc = nc
        self.stack = stack
        self.ops = []
        self.epoch = 0
        self.nsem = 0
        self.semobjs = {}

    def new_sem(self, name):
        s = self.stack.enter_context(self.nc.semaphore(name))
        self.nsem += 1
        return s

    def dma_sem(self, name):
        s = self.new_sem(name)
        d = DmaSem(s, ("dma", name))
        self.semobjs[d.key] = s
        return d

    def add(self, eng, fn, reads=(), writes=(), dsem=None, force=()):
        op = Op()
        op.eng = eng
        op.fn = fn
        op.dsem = dsem
        op.sig = 0
        op.need = False
        op.epoch = self.epoch
        op.dval = 0
        if dsem is not None:
            dsem.count += 16
            op.dval = dsem.count
        op.idx = len(self.ops)
        deps = set()
        for r in reads:
            if r.w is not None:
                deps.add(r.w)
            if r.excl:
                deps.update(o for o in r.r.values() if o.eng != eng)
        for r in writes:
            if r.w is not None:
                deps.add(r.w)
            deps.update(r.r.values())
        for r in reads:
            r.r[_okey(op)] = op
        for r in writes:
            r.w = op
            r.r = {}
        deps.discard(op)
        if eng == "pe":
            deps = {d for d in deps if not (d.eng == "pe" and d.dsem is None)}
        if dsem is not None:
            deps = {d for d in deps if d.dsem is not dsem}
        deps.update(force)
        op.deps = sorted(deps, key=lambda d: d.idx, reverse=(os.environ.get('K_ORD') == 'rev'))
        self.ops.append(op)
        return op

    def emit(self, final_waits=()):
        nc = self.nc
        for op in self.ops:
            for d in op.deps:
                if d.dsem is None:
                    d.need = True
        cnt = {}
        for op in self.ops:
            if op.dsem is None:
                k = (op.eng, op.epoch)
                if op.need:
                    cnt[k] = cnt.get(k, 0) + 1
                    op.sig = cnt[k]
                op.key = k
            else:
                op.key = op.dsem.key
        for k in cnt:
            self.semobjs[k] = self.new_sem("s_%s_%d" % k)
        clock = {e: {} for e in ENGS}
        lastsig = {}
        nwaits = 0
        for op in self.ops:
            ck = clock[op.eng]
            waits = {}
            for d in op.deps:
                key = d.key
                val = d.dval if d.dsem is not None else d.sig
                if ck.get(key, 0) >= val:
                    continue
                waits[key] = max(waits.get(key, 0), val)
                for k, v in d.clock.items():
                    if ck.get(k, 0) < v:
                        ck[k] = v
                ck[key] = max(ck.get(key, 0), val)
            op.waits = list(waits.items())
            nwaits += len(op.waits)
            snap = dict(ck)
            if op.dsem is None:
                if op.sig:
                    lastsig[op.key] = op.sig
                if op.key in lastsig:
                    snap[op.key] = max(snap.get(op.key, 0), lastsig[op.key])
            op.clock = snap
        byeng = {e: [] for e in ENGS}
        for op in self.ops:
            byeng[op.eng].append(op)
        self.stats = {e: len(v) for e, v in byeng.items()}
        self.stats["waits"] = nwaits
        self.stats["sems"] = self.nsem
        semobjs = self.semobjs

        def run(name, e):
            for op in byeng[name]:
                for k, v in op.waits:
                    e.wait_ge(semobjs[k], v)
                ins = op.fn(e)
                if op.dsem is not None:
                    ins.then_inc(op.dsem.sem, 16)
                elif op.sig:
                    ins.then_inc(semobjs[op.key], 1)
            if name == "sp":
                for d in final_waits:
                    e.wait_ge(d.sem, d.count)

        with nc.Block() as block:
            @block.tensor
            def _(e):
                run("pe", e)

            @block.scalar
            def _(e):
                run("act", e)

            @block.vector
            def _(e):
                run("dve", e)

            @block.gpsimd
            def _(e):
                run("pool", e)

            @block.sync
            def _(e):
                run("sp", e)


def _col_layout():
    lay = {}
    off = 0
    for name, n in (("g_attn0", 8), ("g_ffn0", 8), ("g_attn1", 8), ("g_ffn1", 8),
                    ("cw0", 44 * 3), ("cb0", 44), ("cw1", 44 * 3), ("cb1", 44),
                    ("ln_g", 16), ("ln_b", 16), ("kdec", 4), ("qdec", 4), ("flag", 1), ("one", 1)):
        lay[name] = (off, n)
        off += n
    return lay, off


COLS, NCOLS = _col_layout()

def _page_table():
    pages = []
    idx = {}

    def add(group, w, kc0, kcn, segs, g):
        idx.setdefault(group, []).append(len(pages))
        pages.append((w, kc0, kcn, segs, g))

    for i in range(9):
        add("l0_in", "w_in", 0, 8, [(512 * i, 512)], "g_attn0")
    for cg in range(2):
        for (k0, kn) in ((0, 8), (8, 4)):
            add("l0_out", "w_out", k0, kn, [(512 * cg, 512)], None)
    for l in range(2):
        for k in range(11):
            add("f%d_up" % l, "w_up%d" % l, 0, 8, [(256 * k, 256), (2816 + 256 * k, 256)], "g_ffn%d" % l)
        for cg in range(2):
            for (k0, kn) in ((0, 8), (8, 8), (16, 6)):
                add("f%d_dn" % l, "w_dn%d" % l, k0, kn, [(512 * cg, 512)], None)
    for i in range(8):
        add("l1_in", "c_in", 0, 8, [(512 * i, 512)], "g_attn1")
    for cg in range(2):
        for (k0, kn) in ((0, 8), (8, 8)):
            add("l1_out", "c_out", k0, kn, [(512 * cg, 512)], None)
    return pages, idx


PAGES, PIDX = _page_table()
WSHAPES = {"w_in": (1024, 4608), "w_out": (1536, 1024), "w_up0": (1024, 5632), "w_up1": (1024, 5632),
           "w_dn0": (2816, 1024), "w_dn1": (2816, 1024), "c_in": (1024, 4096), "c_out": (2048, 1024)}

ARENA_BYTES = 77312
L0_LAY = {"va": (0, 8192), "sg": (8192, 8192), "kdec": (16384, 4096), "qaT": (20480, 4096),
          "kaT": (24576, 4096), "qbT": (28672, 4096), "qa_tm": (32768, 2048), "ka_tm": (34816, 2048),
          "S_bf": (36864, 4096), "y_tm": (40960, 6144), "yT": (47104, 12288), "p32": (59392, 5120),
          "pb": (64512, 2560), "o_sb": (67072, 2048), "t_sb": (69120, 2048), "sTm": (71168, 2048),
          "rtmp": (73216, 4096)}
FFN_LAY = {"aT": (0, 22528), "acc": (22528, 8192)}
SGU_LAY = {"uT": (0, 16384), "vg": (16384, 16384), "yT2": (32768, 16384), "m4": (49152, 4096)}
PRE_LAY = {"st32": (0, 32768), "st16": (32768, 16384), "bias": (49152, 20480)}


def build_program(n_warm, n_main, stop_stage=5):
    NT = n_warm + 1 + n_main
    nc = bass.Bass("TRN2", target_bir_lowering=False)
    din = lambda n, sh: nc.dram_tensor(n, list(sh), F32, kind="ExternalInput").ap()
    x_d = din("x", (NT * 512, 1024))
    rot_d = din("rot", (NT * 512, 256))
    w_d = {n: din(n, sh) for n, sh in WSHAPES.items()}
    cols_d = din("cols", (P, NCOLS))
    gfin_d = din("gfin", (P, 1024))
    ident_d = din("ident", (P, P))
    bias_d = din("biasT", (P, 8 * 5 * 128))
    dq_d = din("dq", (P, 4 * 128))
    wsT_d = din("wsT", (P, 8 * 128))
    mask_d = din("maskT", (P, 128))
    bsb_d = din("bsb", (P, 8 * 128))
    out_d = nc.dram_tensor("out", [n_main * 512, 1024], F32, kind="ExternalOutput").ap()
    scr_d = nc.dram_tensor("wscr", [len(PAGES), P, 4096], BF16, kind="Internal").ap()

    G128 = [float((1.0 - 2.0 ** (-5 - h)) ** 128) for h in range(4)]

    with ExitStack() as st:
        S = Sched(nc, st)
        sb = lambda n, sh, dt: st.enter_context(nc.sbuf_tensor("sb_" + n, list(sh), dt))
        ps = st.enter_context(nc.psum_tensor("ps", [P, 4096], F32))

        hbuf = [sb("h%d" % b, (P, 4, 1024), F32) for b in range(2)]
        hres = [[Res("h%d_%d" % (b, s)) for s in range(4)] for b in range(2)]
        hnT = sb("hnT", (P, 8, 512), BF16)
        hnT_res = [Res("hnT%d" % s) for s in range(4)]
        hn_tm = [sb("hn_tm%d" % k, (P, 1024), BF16) for k in range(2)]
        hn_tm_res = [Res("hn_tm%d" % k) for k in range(2)]
        junk = sb("junk", (P, 1024), BF16)
        wslot = [sb("wslot%d" % k, (P, 4096), BF16) for k in range(NSLOT)]
        wslot_res = [Res("wslot%d" % k) for k in range(NSLOT)]
        wslot_sem = [S.dma_sem("wsl%d" % k) for k in range(NSLOT)]
        cols = sb("cols", (P, NCOLS), F32)
        gfin = sb("gfin", (P, 1024), F32)
        ident = sb("ident", (P, P), BF16)
        EB = sb("EB", (P, 8, 640), BF16)
        Dq = sb("Dq", (P, 4, 128), F32)
        wTm = sb("wTm", (P, 8, 128), BF16)
        Rt = sb("Rt", (P, 16, 128), F32)
        ones_bf = sb("ones_bf", (P, P), BF16)
        rot = sb("rot", (P, 4, 256), F32)
        rot_res = Res("rot")
        kT = sb("kT", (P, 4, 1024), BF16)
        kT_res = [[Res("kT%d_%d" % (hf, oc)) for oc in range(4)] for hf in range(2)]
        Vr = sb("Vr", (P, 8, 8, 65), BF16)
        V_res = [[Res("V%d_%d" % (hf, s)) for s in range(4)] for hf in range(2)]
        Sst = sb("Sst", (P, 4, 256), F32)
        S_res = [Res("S%d" % h) for h in range(4)]
        zc = [sb("zc%d" % l, (P, 44, 2), F32) for l in range(2)]
        zc_res = [[Res("zc%d_%d" % (l, c)) for c in range(44)] for l in range(2)]
        NSTAT = 12
        stat = [sb("stat%d" % i, (P, 16), F32) for i in range(NSTAT)]
        stat_res = [Res("stat%d" % i) for i in range(NSTAT)]
        stat_ptr = [0]
        arena = sb("arena", (P, ARENA_BYTES // 2), BF16)
        const_res = Res("const")
        setup_res = Res("setup")

        def colap(name, i=0, n=1):
            o, _ = COLS[name]
            return cols[:, o + i:o + i + n]

        live = []

        def carve(lay, name, dtype, shape):
            lo, nb = lay[name]
            hi = lo + nb
            olds = [r for (a, b, r) in live if a < hi and b > lo]
            keep = []
            for (a, b, r) in live:
                if a < hi and b > lo:
                    if a < lo:
                        keep.append((a, lo, r))
                    if b > hi:
                        keep.append((hi, b, r))
                else:
                    keep.append((a, b, r))
            live[:] = keep
            res = Res(name, inherit=olds)
            live.append((lo, hi, res))
            ap = arena[:, lo // 2:hi // 2]
            if dtype == F32:
                ap = ap.bitcast(F32)
            return ap, res

        def split_res(res, n):
            kids = [Res("%s_%d" % (res.name, i), inherit=[res]) for i in range(n)]
            res.kids.extend(kids)
            return kids

        psq = [Res("psb%d" % i) for i in range(8)]
        for r_ in psq:
            r_.excl = True
        psptr = [0]

        def palloc(nq=4):
            i = psptr[0]
            psptr[0] = (i + 1) % 8
            return i * 512, [psq[i]]

        load = {"act": 0.0, "dve": 0.0}

        def pick(n):
            e = "act" if load["act"] <= load["dve"] else "dve"
            return e

        def cost(eng, n):
            if eng == "act":
                load["act"] += n / 1.4 + 160
            elif eng == "dve":
                load["dve"] += n / 0.96 + 60

        def act(out, in_, func, reads, writes, n=512, **kw):
            cost("act", n)
            return S.add("act", lambda e: e.activation(out=out, in_=in_, func=func, **kw), reads, writes)

        def copy(eng, out, in_, reads, writes, n=512):
            cost(eng, n)
            if eng == "act":
                return S.add("act", lambda e: e.activation(out=out, in_=in_, func=AF.Copy), reads, writes)
            return S.add(eng, lambda e: e.tensor_copy(out=out, in_=in_), reads, writes)

        def tt(out, in0, in1, op, reads, writes, n=512, eng="dve"):
            cost(eng, n)
            return S.add(eng, lambda e: e.tensor_tensor(out=out, in0=in0, in1=in1, op=op), reads, writes)

        def ts(out, in0, s1, s2, op0, op1, reads, writes, n=64):
            cost("dve", n)
            if s2 is None:
                return S.add("dve", lambda e: e.tensor_scalar(out=out, in0=in0, scalar1=s1, scalar2=None, op0=op0), reads, writes)
            return S.add("dve", lambda e: e.tensor_scalar(out=out, in0=in0, scalar1=s1, scalar2=s2, op0=op0, op1=op1), reads, writes)

        def stt(out, in0, scalar, in1, op0, op1, reads, writes, n=512):
            cost("dve", n)
            return S.add("dve", lambda e: e.scalar_tensor_tensor(out=out, in0=in0, scalar=scalar, in1=in1, op0=op0, op1=op1), reads, writes)

        def mm(out, lhsT, rhs, start, stop, reads, writes, force=()):
            return S.add("pe", lambda e: e.matmul(out, lhsT=lhsT, rhs=rhs, start=start, stop=stop), reads, writes, force=force)

        def trp(out, in_, reads, writes):
            return S.add("pe", lambda e: e.transpose(out, in_, ident[:]), list(reads) + [setup_res], writes)

        def dma(eng, out, in_, reads, writes, dsem):
            return S.add(eng, lambda e: e.dma_start(out=out, in_=in_), reads, writes, dsem=dsem)

        def new_stat():
            i = stat_ptr[0] % NSTAT
            stat_ptr[0] += 1
            return stat[i], stat_res[i]

        if CUT >= 1:
            cst = S.dma_sem("cst")
            cops = []
            biasT, r_bias = carve(PRE_LAY, "bias", F32, None)
            for dst, src in ((cols[:], cols_d), (gfin[:], gfin_d), (Dq[:].rearrange("p a b -> p (a b)"), dq_d)):
                cops.append(dma("sp", dst, src, [], [const_res], cst))
            cops.append(dma("sp", biasT, bias_d, [], [r_bias], cst))
            a_st32, r_st32 = carve(PRE_LAY, "st32", F32, None)
            a_st16, r_st16 = carve(PRE_LAY, "st16", BF16, None)
            wsT_f = a_st32[:, 0:1024]
            mask_f = a_st32[:, 1024:1152]
            bsb_f = a_st32[:, 1152:2176]
            rs_f = a_st32[:, 2176:3200]
            identf = a_st32[:, 3200:3328]
            cops.append(dma("sp", identf, ident_d, [], [r_st32], cst))
            cops.append(dma("sp", wsT_f, wsT_d, [], [r_st32], cst))
            cops.append(dma("sp", mask_f, mask_d, [], [r_st32], cst))
            cops.append(dma("sp", bsb_f, bsb_d, [], [r_st32], cst))
            for o in cops:
                o.dval = cst.count
            copy("dve", ident[:], identf, [r_st32], [setup_res], n=128)
            S.add("dve", lambda e: e.memset(ones_bf[:], 1.0), [], [setup_res])
            for h in range(8):
                act(EB[:, h, :], biasT[:, h * 640:(h + 1) * 640], AF.Exp, [r_bias], [setup_res], n=640)
            tt(wTm[:].rearrange("p g i -> p g i"), wsT_f.rearrange("p (g i) -> p g i", g=8),
               mask_f.unsqueeze(1).broadcast_to([P, 8, 128]), ALU.mult, [r_st32], [setup_res], n=1024)
            for g in range(8):
                pc, pr = palloc(1)
                mm(ps[:, pc:pc + 128], ones_bf[:], wTm[:, g, :], True, True, [setup_res], pr)
                copy("act", rs_f[:, g * 128:(g + 1) * 128], ps[:, pc:pc + 128], pr, [r_st32], n=128)
            for c in range(16):
                g = c // 2
                stt(Rt[:, c, :], rs_f[:, g * 128:(g + 1) * 128], colap("ln_b", c), bsb_f[:, g * 128:(g + 1) * 128],
                    ALU.mult, ALU.add, [r_st32, const_res], [setup_res], n=128)
            S.add("dve", lambda e: e.memset(kT[:], 0.0), [], [r for hf in kT_res for r in hf])
            S.add("dve", lambda e: e.memset(Vr[:], 0.0), [], [r for hf in V_res for r in hf])
            S.add("dve", lambda e: e.memset(Sst[:], 0.0), [], S_res)
            for l in range(2):
                S.add("dve", lambda e, l=l: e.memset(zc[l][:], 0.0), [], zc_res[l])


        if CUT >= 2:
            st32 = [a_st32[:, k * 4096:(k + 1) * 4096] for k in range(2)]
            st16 = [a_st16[:, k * 4096:(k + 1) * 4096] for k in range(2)]
            st32_res = split_res(r_st32, 2)
            st16_res = split_res(r_st16, 2)
            ld_sem = [S.dma_sem("pld%d" % k) for k in range(2)]
            stq_sem = [S.dma_sem("pst%d" % k) for k in range(2)]
            scr_res = [Res("scr%d" % i) for i in range(len(PAGES))]
            for pi, (wn, kc0, kcn, segs, gname) in enumerate(PAGES):
                k = pi % 2
                wv = w_d[wn].rearrange("(kc p) n -> p kc n", p=P)
                s3 = st32[k].rearrange("p (kc n) -> p kc n", n=512)
                lops = []
                co = 0
                for (c0, ncol) in segs:
                    lops.append(dma("sp", s3[:, 0:kcn, co:co + ncol], wv[:, kc0:kc0 + kcn, c0:c0 + ncol], [], [st32_res[k]], ld_sem[k]))
                    co += ncol
                for o in lops:
                    o.dval = ld_sem[k].count
                d3 = st16[k].rearrange("p (kc n) -> p kc n", n=512)
                for kc in range(kcn):
                    eng = pick(512)
                    if gname is None:
                        copy(eng, d3[:, kc, :], s3[:, kc, :], [st32_res[k]], [st16_res[k]])
                    elif eng == "act":
                        act(d3[:, kc, :], s3[:, kc, :], AF.Copy, [st32_res[k], const_res], [st16_res[k]], scale=colap(gname, kc0 + kc))
                    else:
                        ts(d3[:, kc, :], s3[:, kc, :], colap(gname, kc0 + kc), None, ALU.mult, None, [st32_res[k], const_res], [st16_res[k]], n=512)
                dma("sp", scr_d[pi, :, 0:kcn * 512], st16[k][:, 0:kcn * 512], [st16_res[k]], [scr_res[pi]], stq_sem[k])


        seq = []
        for t in range(NT):
            if t < n_warm:
                pl = [PIDX["l0_in"][i] for i in (1, 2, 3)]
                if t == n_warm - 1:
                    pl += [PIDX["l0_in"][7], PIDX["l0_in"][8]]
            else:
                pl = list(PIDX["l0_in"]) + list(PIDX["l0_out"])
                if stop_stage >= 2:
                    pl += PIDX["f0_up"] + PIDX["f0_dn"]
                if stop_stage >= 3:
                    pl += PIDX["l1_in"] + PIDX["l1_out"]
                if stop_stage >= 4:
                    pl += PIDX["f1_up"] + PIDX["f1_dn"]
            seq += pl
        wstate = {"issued": 0, "used": 0}

        def w_prefetch(upto):
            while wstate["issued"] < min(upto, len(seq)):
                i = wstate["issued"]
                pi = seq[i]
                kcn = PAGES[pi][2]
                k = i % NSLOT
                dma("sp", wslot[k][:, 0:kcn * 512], scr_d[pi, :, 0:kcn * 512], [scr_res[pi]], [wslot_res[k]], wslot_sem[k])
                wstate["issued"] += 1

        def w_group(pis):
            i0 = wstate["used"]
            assert len(pis) <= NSLOT - 1
            w_prefetch(i0 + NSLOT)
            outl = []
            for m, expect in enumerate(pis):
                i = i0 + m
                assert seq[i] == expect, (i, seq[i], expect)
                k = i % NSLOT
                outl.append((wslot[k][:].rearrange("p (kc n) -> p kc n", n=512), wslot_res[k]))
            wstate["used"] += len(pis)
            return outl

        def w_next(expect):
            return w_group([expect])[0]

        x_sem = [S.dma_sem("xld%d" % b) for b in range(2)]
        rot_sem = S.dma_sem("rotld")
        out_sem = [S.dma_sem("ost%d" % b) for b in range(2)]

        def load_x(t):
            b = t % 2
            ops = []
            for s in range(4):
                ops.append(dma("pool", hbuf[b][:, s, :], x_d[t * 512 + s * 128:t * 512 + (s + 1) * 128, :], [], [hres[b][s]], x_sem[b]))
            for o in ops:
                o.dval = x_sem[b].count

        def load_rot(t):
            dma("pool", rot[:], rot_d[t * 512:(t + 1) * 512, :].rearrange("(s p) c -> p s c", p=P), [], [rot_res], rot_sem)

        def rms_stats(b, reads_extra=()):
            sa, sr = new_stat()
            S.add("dve", lambda e: e.memset(sa[:, 0:4], 0.0), [], [sr])
            for s in range(4):
                act(junk[:], hbuf[b][:, s, :], AF.Square, [hres[b][s]], [sr], n=1024, accum_out=sa[:, s:s + 1])
            ts(sa[:, 4:8], sa[:, 0:4], 1.0 / 1024, EPS, ALU.mult, ALU.add, [sr], [sr])
            act(sa[:, 8:12], sa[:, 4:8], AF.Sqrt, [sr], [sr], n=4)
            S.add("dve", lambda e: e.reciprocal(out=sa[:, 12:16], in_=sa[:, 8:12]), [sr], [sr])
            return sa, sr

        def rmsnorm_T(b):
            sa, sr = rms_stats(b)
            for s in range(4):
                k = s % 2
                act(hn_tm[k][:], hbuf[b][:, s, :], AF.Copy, [hres[b][s], sr], [hn_tm_res[k]], n=1024, scale=sa[:, 12 + s:13 + s])
                pc, pr = palloc(4)
                pv = ps[:, pc:pc + 512].bitcast(BF16)
                for kc in range(8):
                    trp(pv[:, kc * 128:(kc + 1) * 128], hn_tm[k][:, kc * 128:(kc + 1) * 128], [hn_tm_res[k]], pr)
                eng = pick(1024)
                copy(eng, hnT[:, :, s * 128:(s + 1) * 128], pv.rearrange("p (a b) -> p a b", a=8), pr, [hnT_res[s]], n=1024)

        def page_tokmajor(pi, evac, kcn=8, lhs=None, lhs_res=None, acc_first=True):
            wv, wr = w_next(pi)
            for s in range(4):
                pc, pr = palloc(4)
                for kc in range(kcn):
                    mm(ps[:, pc:pc + 512], hnT[:, kc, s * 128:(s + 1) * 128], wv[:, kc, :], kc == 0, kc == kcn - 1,
                       [hnT_res[s], wr], pr)
                evac(s, pc, pr)

        def page_featmajor(pi, evac):
            wv, wr = w_next(pi)
            for oc in range(4):
                pc, pr = palloc(4)
                for kc in range(8):
                    mm(ps[:, pc:pc + 512], wv[:, kc, oc * 128:(oc + 1) * 128], hnT[:, kc, :], kc == 0, kc == 7,
                       list(hnT_res) + [wr], pr)
                evac(oc, pc, pr)

        def proj_residual(pages, kgroups, lhs_of, lhs_res_of, b):
            for cg in range(2):
                wvs = w_group(pages[cg])
                nk = sum(kgroups)
                for s in range(4):
                    pc, pr = palloc(4)
                    kk = 0
                    for (wv, wr), kn in zip(wvs, kgroups):
                        for kc in range(kn):
                            mm(ps[:, pc:pc + 512], lhs_of(kk, s), wv[:, kc, :], kk == 0, kk == nk - 1,
                               list(lhs_res_of(kk, s)) + [wr], pr)
                            kk += 1
                    tt(hbuf[b][:, s, cg * 512:(cg + 1) * 512], hbuf[b][:, s, cg * 512:(cg + 1) * 512], ps[:, pc:pc + 512],
                       ALU.add, pr + [hres[b][s]], [hres[b][s]])

        def rotary_evac(pc, pr, s, dst, dst_res, rt, rt_res):
            z = ps[:, pc:pc + 512].rearrange("p (h d) -> p h d", h=4)
            A = rt[0].rearrange("p (h d) -> p h d", h=4)
            B = rt[1].rearrange("p (h d) -> p h d", h=4)
            cc = rot[:, s, 0:128].unsqueeze(1).broadcast_to([P, 4, 128])
            sn_lo = rot[:, s, 128:192].unsqueeze(1).broadcast_to([P, 4, 64])
            sn_hi = rot[:, s, 192:256].unsqueeze(1).broadcast_to([P, 4, 64])
            tt(A, z, cc, ALU.mult, pr + [rot_res], [rt_res[0]])
            tt(B[:, :, 0:64], z[:, :, 64:128], sn_lo, ALU.mult, pr + [rot_res], [rt_res[1]], n=256)
            tt(B[:, :, 64:128], z[:, :, 0:64], sn_hi, ALU.mult, pr + [rot_res], [rt_res[1]], n=256)
            tt(dst, rt[0], rt[1], ALU.add, [rt_res[0], rt_res[1]], [dst_res])

        def l0_inproj(t, kind, bufs):
            half = t % 2
            full = kind != "warm"
            last_warm = (t == n_warm - 1)
            L = bufs

            def ev_q(s, pc, pr):
                k = s % 2
                rotary_evac(pc, pr, s, L["qa_tm"][:, k, :], L["qa_tm_r"][k], L["rtmp"], L["rtmp_r"])
                qc, qr = palloc(2)
                qv = ps[:, qc:qc + 256].bitcast(BF16)
                for h in range(4):
                    trp(qv[:, h * 128:(h + 1) * 128], L["qa_tm"][:, k, h * 128:(h + 1) * 128], [L["qa_tm_r"][k]], qr)
                copy(pick(512), L["qaT"][:, :, s * 128:(s + 1) * 128], qv.rearrange("p (h d) -> p h d", h=4), qr, [L["qaT_r"][s]])

            def ev_k(s, pc, pr):
                k = s % 2
                rotary_evac(pc, pr, s, L["ka_tm"][:, k, :], L["ka_tm_r"][k], L["rtmp"], L["rtmp_r"])
                if full:
                    qc, qr = palloc(2)
                    qv = ps[:, qc:qc + 256].bitcast(BF16)
                    for h in range(4):
                        trp(qv[:, h * 128:(h + 1) * 128], L["ka_tm"][:, k, h * 128:(h + 1) * 128], [L["ka_tm_r"][k]], qr)
                    copy(pick(512), L["kaT"][:, :, s * 128:(s + 1) * 128], qv.rearrange("p (h d) -> p h d", h=4), qr, [L["kaT_r"][s]])
                o, _ = COLS["kdec"]
                tt(L["kdec"][:, s, :].rearrange("p (h d) -> p h d", h=4), L["ka_tm"][:, k, :].rearrange("p (h d) -> p h d", h=4),
                   cols[:, o:o + 4].unsqueeze(2).broadcast_to([P, 4, 128]), ALU.mult, [L["ka_tm_r"][k], const_res], [L["kdec_r"][s]])

            def ev_v(pg):
                def f(s, pc, pr):
                    copy(pick(512), L["va"][:, s, pg * 512:(pg + 1) * 512], ps[:, pc:pc + 512], pr, [L["va_r"][s][pg]])
                return f

            def ev_g(pg):
                def f(s, pc, pr):
                    act(L["sg"][:, s, pg * 512:(pg + 1) * 512], ps[:, pc:pc + 512], AF.Silu, pr, [L["sg_r"][s][pg]])
                return f

            def ev_qb(oc, pc, pr):
                copy(pick(512), L["qbT"][:, oc, :], ps[:, pc:pc + 512], pr, [L["qbT_r"][oc]])

            def ev_kb(oc, pc, pr):
                copy(pick(512), kT[:, oc, half * 512:(half + 1) * 512], ps[:, pc:pc + 512], pr, [kT_res[half][oc]])

            def ev_vb(s, pc, pr):
                dst = Vr[:, half * 4 + s, :, 0:64]
                src = ps[:, pc:pc + 512].rearrange("p (h d) -> p h d", h=8)
                if kind == "main":
                    copy(pick(512), dst, src, pr, [V_res[half][s]])
                else:
                    act(dst, src, AF.Copy, pr + [const_res], [V_res[half][s]], scale=colap("flag"))

            pin = PIDX["l0_in"]
            if full:
                page_tokmajor(pin[0], ev_q)
            page_tokmajor(pin[1], ev_k)
            page_tokmajor(pin[2], ev_v(0))
            page_tokmajor(pin[3], ev_v(1))
            if full:
                page_tokmajor(pin[4], ev_g(0))
                page_tokmajor(pin[5], ev_g(1))
                page_featmajor(pin[6], ev_qb)
            if full or last_warm:
                src = colap("one") if kind == "main" else colap("flag")
                S.add("dve", lambda e: e.tensor_copy(out=Vr[:, half * 4:half * 4 + 4, :, 64:65],
                                                     in_=src.unsqueeze(1).unsqueeze(1).broadcast_to([P, 4, 8, 1])),
                      [const_res], V_res[half])
                page_featmajor(pin[7], ev_kb)
                page_tokmajor(pin[8], ev_vb)

        def retention_state(s, L, need_bf, nxt):
            for h in range(4):
                pc, pr = palloc(2)
                mm(ps[:, pc:pc + 256], L["kdec"][:, s, h * 128:(h + 1) * 128], L["va"][:, s, h * 256:(h + 1) * 256], True, True,
                   [L["kdec_r"][s], L["va_r"][s][h // 2]], pr)
                stt(Sst[:, h, :], Sst[:, h, :], G128[h], ps[:, pc:pc + 256], ALU.mult, ALU.add, pr + [S_res[h]], [S_res[h]], n=256)
                if need_bf:
                    copy("act", L["S_bf"][:, nxt, h, :], Sst[:, h, :], [S_res[h]], [L["S_bf_r"][nxt][h]], n=256)

        def l0_mixer(t, kind, b):
            half = t % 2
            L = {}
            for name, dt, shp in (("va", BF16, "p (s c) -> p s c"), ("sg", BF16, "p (s c) -> p s c"), ("kdec", BF16, "p (s c) -> p s c"),
                                  ("qaT", BF16, "p (h c) -> p h c"), ("kaT", BF16, "p (h c) -> p h c"), ("qbT", BF16, "p (h c) -> p h c"),
                                  ("qa_tm", BF16, "p (s c) -> p s c"), ("ka_tm", BF16, "p (s c) -> p s c")):
                ap, r = carve(L0_LAY, name, dt, None)
                n1 = {"va": 4, "sg": 4, "kdec": 4, "qaT": 4, "kaT": 4, "qbT": 4, "qa_tm": 2, "ka_tm": 2}[name]
                L[name] = ap.rearrange(shp, **{shp.split("(")[1][0]: n1})
                L[name + "_r"] = split_res(r, n1)
            for name in ("va", "sg"):
                L[name + "_r"] = [split_res(r, 2) for r in L[name + "_r"]]
            ap, r = carve(L0_LAY, "S_bf", BF16, None)
            L["S_bf"] = ap.rearrange("p (k h e) -> p k h e", k=2, h=4)
            L["S_bf_r"] = [split_res(r, 4) for _ in range(2)]
            ap, r = carve(L0_LAY, "rtmp", F32, None)
            L["rtmp"] = [ap[:, 0:512], ap[:, 512:1024]]
            L["rtmp_r"] = split_res(r, 2)
            if kind == "warm":
                l0_inproj(t, kind, L)
                for s in range(4):
                    retention_state(s, L, False, 0)
                return
            ap, r = carve(L0_LAY, "y_tm", BF16, None)
            y_tm = ap.rearrange("p (k c) -> p k c", k=2)
            y_tm_r = [split_res(r, 6) for _ in range(2)]
            ap, r = carve(L0_LAY, "yT", BF16, None)
            yT = ap.rearrange("p (k c) -> p k c", k=12)
            yT_r = [split_res(r, 3) for _ in range(4)]
            ap, r = carve(L0_LAY, "p32", F32, None)
            p32 = ap.rearrange("p (k c) -> p k c", k=2)
            p32_r = split_res(r, 2)
            ap, r = carve(L0_LAY, "pb", BF16, None)
            pb = ap.rearrange("p (k c) -> p k c", k=2)
            pb_r = split_res(r, 2)
            ap, r = carve(L0_LAY, "o_sb", F32, None)
            o_sb = ap.rearrange("p (k c) -> p k c", k=2)
            o_sb_r = split_res(r, 2)
            ap, r = carve(L0_LAY, "t_sb", F32, None)
            t_sb = ap.rearrange("p (k c) -> p k c", k=2)
            t_sb_r = split_res(r, 2)
            ap, r = carve(L0_LAY, "sTm", BF16, None)
            sTm = ap.rearrange("p (k c) -> p k c", k=2)
            sTm_r = split_res(r, 2)
            for h in range(4):
                copy("act", L["S_bf"][:, 0, h, :], Sst[:, h, :], [S_res[h]], [L["S_bf_r"][0][h]], n=256)
            rmsnorm_T(b)
            cut(3)
            l0_inproj(t, kind, L)
            cut(4)
            if t + 1 < NT:
                load_rot(t + 1)
            qd, _ = COLS["qdec"]
            cnt = 0
            for s in range(4):
                cur, nxt = s % 2, (s + 1) % 2
                yk = s % 2
                sl = slice(s * 128, (s + 1) * 128)
                pc, pr = palloc()
                for h in range(4):
                    mm(ps[:, pc + h * 128:pc + (h + 1) * 128], L["kaT"][:, h, sl], L["qaT"][:, h, sl], True, True,
                       [L["kaT_r"][s], L["qaT_r"][s]], pr)
                tt(sTm[:, cur, :].rearrange("p (h i) -> p h i", h=4), ps[:, pc:pc + 512].rearrange("p (h i) -> p h i", h=4), Dq[:],
                   ALU.mult, pr + [const_res], [sTm_r[cur]])
                for hp2 in range(2):
                    oc_, or_ = palloc()
                    for hh in range(2):
                        h = hp2 * 2 + hh
                        mm(ps[:, oc_ + hh * 256:oc_ + (hh + 1) * 256], sTm[:, cur, h * 128:(h + 1) * 128], L["va"][:, s, h * 256:(h + 1) * 256], True, False,
                           [sTm_r[cur], L["va_r"][s][h // 2]], or_)
                        mm(ps[:, oc_ + hh * 256:oc_ + (hh + 1) * 256], L["qaT"][:, h, sl], L["S_bf"][:, cur, h, :], False, True,
                           [L["qaT_r"][s], L["S_bf_r"][cur][h]], or_)
                    for hh in range(2):
                        h = hp2 * 2 + hh
                        k2 = cnt % 2
                        cnt += 1
                        act(o_sb[:, k2, :], ps[:, oc_ + hh * 256:oc_ + (hh + 1) * 256], AF.Copy, or_ + [const_res], [o_sb_r[k2]], n=256,
                            scale=cols[:, qd + h:qd + h + 1])
                        sa, sr = new_stat()
                        S.add("dve", lambda e, sa=sa, k2=k2: e.bn_stats(out=sa[:, 0:6], in_=o_sb[:, k2, :]), [o_sb_r[k2]], [sr])
                        S.add("dve", lambda e, sa=sa: e.bn_aggr(out=sa[:, 6:8], in_=sa[:, 0:6]), [sr], [sr])
                        ts(sa[:, 8:9], sa[:, 7:8], EPS, None, ALU.add, None, [sr], [sr])
                        act(sa[:, 9:10], sa[:, 8:9], AF.Sqrt, [sr], [sr], n=1)
                        S.add("dve", lambda e, sa=sa: e.reciprocal(out=sa[:, 10:11], in_=sa[:, 9:10]), [sr], [sr])
                        stt(t_sb[:, k2, :], o_sb[:, k2, :], sa[:, 6:7], L["sg"][:, s, h * 256:(h + 1) * 256], ALU.subtract, ALU.mult,
                            [o_sb_r[k2], sr, L["sg_r"][s][h // 2]], [t_sb_r[k2]], n=256)
                        act(y_tm[:, yk, h * 256:(h + 1) * 256], t_sb[:, k2, :], AF.Copy, [t_sb_r[k2], sr], [y_tm_r[yk][h]], n=256, scale=sa[:, 10:11])
                for hp2 in range(2):
                    kc_, kr_ = palloc()
                    for hh in range(2):
                        h = hp2 * 2 + hh
                        mm(ps[:, kc_ + hh * 256:kc_ + (hh + 1) * 256], L["kdec"][:, s, h * 128:(h + 1) * 128], L["va"][:, s, h * 256:(h + 1) * 256], True, True,
                           [L["kdec_r"][s], L["va_r"][s][h // 2]], kr_)
                    for hh in range(2):
                        h = hp2 * 2 + hh
                        stt(Sst[:, h, :], Sst[:, h, :], G128[h], ps[:, kc_ + hh * 256:kc_ + (hh + 1) * 256], ALU.mult, ALU.add, kr_ + [S_res[h]], [S_res[h]], n=256)
                        copy("act", L["S_bf"][:, nxt, h, :], Sst[:, h, :], [S_res[h]], [L["S_bf_r"][nxt][h]], n=256)
                cut(5)
                blocks = [(1 - half, j) for j in range(s, 4)] + [(half, j) for j in range(0, s + 1)]
                gbank = {}
                zprev = [None]

                def emit_scores(q, defer_exp=False):
                    cX, rX = palloc()
                    cY, rY = palloc()
                    cZ, rZ = palloc()
                    info = []
                    for idx, (cb, rb) in enumerate(((cX, rX), (cY, rY))):
                        h = 2 * q + idx
                        po = idx * 64
                        for kb, (hf, j) in enumerate(blocks):
                            dst = ps[:, cb + kb * 128:cb + (kb + 1) * 128] if kb < 4 else ps[:, cZ + idx * 128:cZ + (idx + 1) * 128]
                            fz = [zprev[0]] if (kb == 4 and idx == 1) else []
                            o_ = mm(dst, kT[po:po + 64, q, hf * 512 + j * 128:hf * 512 + (j + 1) * 128], L["qbT"][po:po + 64, q, sl],
                                    True, True, [kT_res[hf][q], L["qbT_r"][q]], rb if kb < 4 else rZ, force=fz)
                            if kb == 4 and idx == 0:
                                zprev[0] = o_
                    cut(61)
                    def exp_part():
                      info = []
                      for idx, (cb, rb) in enumerate(((cX, rX), (cY, rY))):
                        h = 2 * q + idx
                        k2 = (2 * q + idx) % 2
                        act(p32[:, k2, 0:512], ps[:, cb:cb + 512], AF.Exp, rb, [p32_r[k2]], n=512, scale=0.125)
                        act(p32[:, k2, 512:640], ps[:, cZ + idx * 128:cZ + (idx + 1) * 128], AF.Exp, rZ, [p32_r[k2]], n=128, scale=0.125)
                        cut(62)
                        tt(pb[:, k2, :], p32[:, k2, :], EB[:, h, :], ALU.mult, [p32_r[k2], setup_res], [pb_r[k2]], n=640)
                        info.append((h, k2))
                      return info
                    if defer_exp:
                        return exp_part
                    return exp_part()

                def emit_pv(q, info):
                    g = q // 2
                    if q % 2 == 0:
                        gbank[g] = palloc()
                    gc, gr = gbank[g]
                    for (h, k2) in info:
                        hh = h % 4
                        for kb, (hf, j) in enumerate(blocks):
                            mm(ps[:, gc + hh * 65:gc + hh * 65 + 65], pb[:, k2, kb * 128:(kb + 1) * 128], Vr[:, hf * 4 + j, h, :], kb == 0, kb == 4,
                               [pb_r[k2], V_res[hf][j]], gr)
                    cut(64)
                    if q % 2 == 1:
                        cut(67)
                        sa, sr = new_stat()
                        ov = ps[:, gc:gc + 260].rearrange("p (h d) -> p h d", h=4)
                        ts(sa[:, 0:4].unsqueeze(2), ov[:, :, 64:65], 1e-30, None, ALU.add, None, gr, [sr])
                        cut(68)
                        S.add("dve", lambda e, sa=sa: e.reciprocal(out=sa[:, 4:8], in_=sa[:, 0:4]), [sr], [sr])
                        cut(65)
                        tt(y_tm[:, yk, 1024 + g * 256:1024 + (g + 1) * 256].rearrange("p (h d) -> p h d", h=4), ov[:, :, 0:64],
                           sa[:, 4:8].unsqueeze(2).broadcast_to([P, 4, 64]), ALU.mult, gr + [sr], [y_tm_r[yk][4 + g]], n=256)
                        cut(66)

                infos = {}
                infos[0] = emit_scores(0)
                for q in range(4):
                    if q + 1 < 4:
                        infos[q + 1] = emit_scores(q + 1, defer_exp=True)
                    emit_pv(q, infos[q] if not callable(infos[q]) else infos[q]())
                    if q + 1 < 4 and callable(infos[q + 1]):
                        infos[q + 1] = infos[q + 1]()
                cut(6)
                for (k0, kn) in ((0, 8), (8, 4)):
                    qc, qr = palloc()
                    qv = ps[:, qc:qc + 512].bitcast(BF16)
                    rr = []
                    for i in range(kn):
                        kc = k0 + i
                        r1 = y_tm_r[yk][kc // 2] if kc < 8 else y_tm_r[yk][4 + (kc - 8) // 2]
                        trp(qv[:, i * 128:(i + 1) * 128], y_tm[:, yk, kc * 128:(kc + 1) * 128], [r1], qr)
                    dst_r = [yT_r[s][0], yT_r[s][1]] if k0 == 0 else [yT_r[s][2]]
                    copy(pick(512), yT[:, k0:k0 + kn, sl], qv[:, 0:kn * 128].rearrange("p (a c) -> p a c", a=kn), qr, dst_r, n=kn * 128)
            cut(7)
            po_ = PIDX["l0_out"]
            proj_residual([[po_[0], po_[1]], [po_[2], po_[3]]], (8, 4),
                          lambda kk, s: yT[:, kk, s * 128:(s + 1) * 128], lambda kk, s: [yT_r[s][kk // 4]], b)

        def conv_ffn(l, b):
            ap, r = carve(FFN_LAY, "aT", BF16, None)
            aT = ap.rearrange("p (c t) -> p c t", c=22)
            aT_r = split_res(r, 22)
            ap, r = carve(FFN_LAY, "acc", F32, None)
            acc = ap.rearrange("p (k t) -> p k t", k=4)
            acc_r = split_res(r, 4)
            rmsnorm_T(b)
            cw, _ = COLS["cw%d" % l]
            cb, _ = COLS["cb%d" % l]
            pair = [0]

            def conv_evac(cidx, pc, pr, k):
                z = ps[:, pc:pc + 512]
                w0 = cols[:, cw + cidx * 3 + 0:cw + cidx * 3 + 1]
                w1 = cols[:, cw + cidx * 3 + 1:cw + cidx * 3 + 2]
                w2 = cols[:, cw + cidx * 3 + 2:cw + cidx * 3 + 3]
                bb = cols[:, cb + cidx:cb + cidx + 1]
                zr = zc_res[l][cidx]
                act(acc[:, k, :], z, AF.Identity, pr + [const_res], [acc_r[k]], scale=w2, bias=bb)
                stt(acc[:, k, 1:512], z[:, 0:511], w1, acc[:, k, 1:512], ALU.mult, ALU.add, pr + [acc_r[k]], [acc_r[k]])
                stt(acc[:, k, 2:512], z[:, 0:510], w0, acc[:, k, 2:512], ALU.mult, ALU.add, pr + [acc_r[k]], [acc_r[k]])
                stt(acc[:, k, 0:1], zc[l][:, cidx, 1:2], w1, acc[:, k, 0:1], ALU.mult, ALU.add, [zr, acc_r[k]], [acc_r[k]], n=1)
                stt(acc[:, k, 0:2], zc[l][:, cidx, 0:2], w0, acc[:, k, 0:2], ALU.mult, ALU.add, [zr, acc_r[k]], [acc_r[k]], n=2)
                copy("act", zc[l][:, cidx, :], z[:, 510:512], pr, [zr], n=2)

            def ev_up(pgk):
                def f(oc, pc, pr):
                    if oc < 2:
                        c = 2 * pgk + oc
                        k = oc
                        conv_evac(c, pc, pr, k)
                        act(acc[:, k, :], acc[:, k, :], AF.Gelu_apprx_tanh, [acc_r[k]], [acc_r[k]])
                    else:
                        c = 2 * pgk + (oc - 2)
                        k = oc
                        conv_evac(22 + c, pc, pr, k)
                        tt(aT[:, c, :], acc[:, oc - 2, :], acc[:, k, :], ALU.mult, [acc_r[oc - 2], acc_r[k]], [aT_r[c]])
                return f

            for k, pi in enumerate(PIDX["f%d_up" % l]):
                page_featmajor(pi, ev_up(k))
            pd = PIDX["f%d_dn" % l]
            proj_residual([pd[0:3], pd[3:6]], (8, 8, 6), lambda kk, s: aT[:, kk, s * 128:(s + 1) * 128], lambda kk, s: [aT_r[kk]], b)

        def sgu_mixer(b):
            ap, r = carve(SGU_LAY, "uT", BF16, None)
            uT = ap.rearrange("p (c t) -> p c t", c=16)
            uT_r = split_res(r, 16)
            ap, r = carve(SGU_LAY, "vg", BF16, None)
            vg = ap.rearrange("p (s c) -> p s c", s=4)
            vg_r = split_res(r, 4)
            ap, r = carve(SGU_LAY, "yT2", BF16, None)
            yT2 = ap.rearrange("p (c t) -> p c t", c=16)
            yT2_r = [split_res(r, 4) for _ in range(4)]
            ap, r = carve(SGU_LAY, "m4", F32, None)
            m4 = ap.rearrange("p (k t) -> p k t", k=2)
            m4_r = split_res(r, 2)
            rmsnorm_T(b)
            pin = PIDX["l1_in"]

            def ev_u(pg):
                def f(oc, pc, pr):
                    act(uT[:, pg * 4 + oc, :], ps[:, pc:pc + 512], AF.Gelu_apprx_tanh, pr, [uT_r[pg * 4 + oc]])
                return f

            def ev_v(pg):
                def f(s, pc, pr):
                    act(vg[:, s, pg * 512:(pg + 1) * 512], ps[:, pc:pc + 512], AF.Gelu_apprx_tanh, pr, [vg_r[s]])
                return f

            for pg in range(4):
                page_featmajor(pin[pg], ev_u(pg))
            for pg in range(4):
                page_tokmajor(pin[4 + pg], ev_v(pg))
            lg, _ = COLS["ln_g"]
            cnt = 0
            for s in range(4):
                sa, sr = new_stat()
                st24 = stat24[s % 2]
                st24_r = stat24_res[s % 2]
                for q in range(4):
                    S.add("dve", lambda e, q=q, s=s, st24=st24: e.bn_stats(out=st24[:, q * 6:(q + 1) * 6], in_=vg[:, s, q * 512:(q + 1) * 512]), [vg_r[s]], [st24_r])
                S.add("dve", lambda e, sa=sa, st24=st24: e.bn_aggr(out=sa[:, 0:2], in_=st24[:].rearrange("p (q d) -> p q d", q=4)), [st24_r], [sr])
                ts(sa[:, 2:3], sa[:, 1:2], EPS, None, ALU.add, None, [sr], [sr])
                act(sa[:, 3:4], sa[:, 2:3], AF.Sqrt, [sr], [sr], n=1)
                S.add("dve", lambda e, sa=sa: e.reciprocal(out=sa[:, 4:5], in_=sa[:, 3:4]), [sr], [sr])
                ts(vg[:, s, :], vg[:, s, :], sa[:, 0:1], sa[:, 4:5], ALU.subtract, ALU.mult, [vg_r[s], sr], [vg_r[s]], n=2048)
                for c4 in range(4):
                    k2 = cnt % 2
                    cnt += 1
                    pc, pr = palloc(4)
                    for i in range(4):
                        c = c4 * 4 + i
                        mm(ps[:, pc + i * 128:pc + (i + 1) * 128], vg[:, s, c * 128:(c + 1) * 128], wTm[:, c // 2, :], True, True,
                           [vg_r[s], setup_res], pr)
                    for i in range(4):
                        c = c4 * 4 + i
                        stt(m4[:, k2, i * 128:(i + 1) * 128], ps[:, pc + i * 128:pc + (i + 1) * 128], cols[:, lg + c:lg + c + 1], Rt[:, c, :],
                            ALU.mult, ALU.add, pr + [setup_res, const_res], [m4_r[k2]], n=128)
                    tt(yT2[:, c4 * 4:(c4 + 1) * 4, s * 128:(s + 1) * 128], m4[:, k2, :].rearrange("p (a t) -> p a t", a=4),
                       uT[:, c4 * 4:(c4 + 1) * 4, s * 128:(s + 1) * 128], ALU.mult, [m4_r[k2]] + uT_r[c4 * 4:(c4 + 1) * 4], [yT2_r[s][c4]])
            po_ = PIDX["l1_out"]
            proj_residual([[po_[0], po_[1]], [po_[2], po_[3]]], (8, 8),
                          lambda kk, s: yT2[:, kk, s * 128:(s + 1) * 128], lambda kk, s: [yT2_r[s][kk // 4]], b)

        stat24 = [sb("stat24_%d" % i, (P, 24), F32) for i in range(2)]
        stat24_res = [Res("stat24_%d" % i) for i in range(2)]

        load_x(0)
        load_rot(0)
        if CUT < 3:
            for s_ in range(4):
                dma("pool", out_d[s_ * 128:(s_ + 1) * 128, :], hbuf[0][:, s_, :], [hres[0][s_]], [], out_sem[0])
        else:
            w_prefetch(NSLOT - 1)
        try:
            for t in range(NT if CUT >= 3 else 0):
                S.epoch = t + 1
                b = t % 2
                kind = "warm" if t < n_warm else ("halo" if t == n_warm else "main")
                if t + 1 < NT:
                    load_x(t + 1)
                if kind == "warm":
                    rmsnorm_T(b)
                    l0_mixer(t, kind, b)
                    load_rot(t + 1)
                    continue
                l0_mixer(t, kind, b)
                if stop_stage >= 2:
                    conv_ffn(0, b)
                if kind == "halo":
                    for s in range(4):
                        ts(hbuf[b][:, s, :], hbuf[b][:, s, :], colap("flag"), None, ALU.mult, None, [hres[b][s], const_res], [hres[b][s]], n=1024)
                    if stop_stage < 3:
                        continue
                if stop_stage >= 3:
                    sgu_mixer(b)
                if stop_stage >= 4:
                    conv_ffn(1, b)
                if kind == "halo":
                    continue
                if stop_stage >= 5:
                    sa, sr = rms_stats(b)
                    for s in range(4):
                        stt(hbuf[b][:, s, :], hbuf[b][:, s, :], sa[:, 12 + s:13 + s], gfin[:], ALU.mult, ALU.mult,
                            [hres[b][s], sr, const_res], [hres[b][s]], n=1024)
                mt = t - n_warm - 1
                oops = []
                for s in range(4):
                    oops.append(dma("pool", out_d[mt * 512 + s * 128:mt * 512 + (s + 1) * 128, :], hbuf[b][:, s, :], [hres[b][s]], [], out_sem[b]))
                for o in oops:
                    o.dval = out_sem[b].count

        except _Stop:
            for s_ in range(4):
                dma("pool", out_d[s_ * 128:(s_ + 1) * 128, :], hbuf[0][:, s_, :], [hres[0][s_]], [], out_sem[0])
        assert CUT < 99 or wstate["used"] == len(seq), (wstate, len(seq))
        S.emit(final_waits=out_sem)
        build_program.stats = dict(S.stats)
    return nc


def _const_tables():
    f64 = np.float64
    gam = np.array([1.0 - 2.0 ** (-5 - h) for h in range(4)], f64)
    i = np.arange(128)
    ci = i // 64
    scale = 128.0 ** -0.5
    dq = np.zeros((128, 4, 128), f64)
    for h in range(4):
        g = gam[h]
        jj, ii = np.meshgrid(i, i, indexing="ij")
        same = (jj // 64) == (ii // 64)
        earlier = (jj // 64) < (ii // 64)
        Dm = np.where(same, g ** np.abs(ii - jj), np.where(earlier, g ** np.maximum(ii - jj, 0), 0.0))
        dq[:, h, :] = scale * Dm / (g ** (ii + 1.0))
    kdec = np.stack([scale * gam[h] ** (127.0 - i) for h in range(4)], axis=1)
    qdec = np.stack([gam[h] ** (i + 1.0) for h in range(4)], axis=1)
    return dq.astype(np.float32), kdec.astype(np.float32), qdec.astype(np.float32)


def _rel_index():
    i = np.arange(128)[None, :]
    j = np.arange(640)[:, None]
    ci = i // 64
    jc = j // 64
    inband = (jc >= ci) & (jc <= ci + 8)
    qpos = (i % 64) + 512
    kpos = j - 64 * ci
    rel = np.clip(qpos - kpos, -128, 128) + 128
    return rel, inband


def _rot_table(pos):
    f32 = np.float32
    lin = np.linspace(0.0, 1.0, 64, dtype=f32)
    inv = (f32(1.0) / np.power(f32(10000.0), lin).astype(f32)).astype(f32)
    ang = (pos.astype(f32)[:, None] * inv[None, :]).astype(f32).astype(np.float64)
    c = np.cos(ang)
    s = np.sin(ang)
    return np.concatenate([c, c, -s, s], axis=1).astype(f32)


def _shared_inputs(inp):
    f32 = np.float32
    sh = {}
    sh["w_in"] = np.ascontiguousarray(inp["ab_w_in"][0], f32)
    sh["w_out"] = np.ascontiguousarray(inp["ab_w_out"][0], f32)
    sh["w_up0"] = np.ascontiguousarray(inp["ffn_w_up"][0], f32)
    sh["w_up1"] = np.ascontiguousarray(inp["ffn_w_up"][1], f32)
    sh["w_dn0"] = np.ascontiguousarray(inp["ffn_w_down"][0], f32)
    sh["w_dn1"] = np.ascontiguousarray(inp["ffn_w_down"][1], f32)
    sh["c_in"] = np.ascontiguousarray(inp["c_w_in"][0], f32)
    sh["c_out"] = np.ascontiguousarray(inp["c_w_out"][0], f32)
    dq, kdec, qdec = _const_tables()
    cols = np.zeros((P, NCOLS), f32)

    def put(name, arr):
        o, n = COLS[name]
        cols[:, o:o + n] = np.asarray(arr, f32).reshape(P, n)

    colmajor = lambda v: np.asarray(v, f32).reshape(-1, P).T
    put("g_attn0", colmajor(inp["attn_norm_g"][0]))
    put("g_ffn0", colmajor(inp["ffn_norm_g"][0]))
    put("g_attn1", colmajor(inp["attn_norm_g"][1]))
    put("g_ffn1", colmajor(inp["ffn_norm_g"][1]))
    for l in range(2):
        cw = np.asarray(inp["ffn_conv_w"][l], f32)
        put("cw%d" % l, cw.T.reshape(44, P, 3).transpose(1, 0, 2).reshape(P, 44 * 3))
        put("cb%d" % l, colmajor(inp["ffn_conv_b"][l]))
    put("ln_g", colmajor(inp["c_ln_g"][0]))
    put("ln_b", colmajor(inp["c_ln_b"][0]))
    put("kdec", kdec)
    put("qdec", qdec)
    put("one", np.ones((P, 1), f32))
    sh["cols"] = cols
    sh["gfin"] = np.ascontiguousarray(np.broadcast_to(np.asarray(inp["final_norm_g"], f32)[None, :], (P, 1024)))
    sh["ident"] = np.eye(P, dtype=f32)
    rel, inband = _rel_index()
    rb = np.asarray(inp["ab_rel_bias"][0], f32)
    B = np.where(inband[None], rb[:, rel], f32(-1e30)).astype(f32)
    sh["biasT"] = np.ascontiguousarray(B.reshape(8, 5, 128, 128).transpose(2, 0, 1, 3).reshape(P, 8 * 5 * 128))
    sh["dq"] = np.ascontiguousarray(dq.reshape(P, 4 * 128))
    ws = np.asarray(inp["c_w_s"][0], f32)
    sh["wsT"] = np.ascontiguousarray(ws.transpose(2, 0, 1).reshape(P, 8 * 128))
    ii = np.arange(128)
    sh["maskT"] = ((ii[:, None] // 64) <= (ii[None, :] // 64)).astype(f32)
    bs = np.asarray(inp["c_b_s"][0], f32)
    sh["bsb"] = np.ascontiguousarray(np.broadcast_to(bs.reshape(1, 8 * 128), (P, 8 * 128)))
    return sh


def run_module(inp, seg, n_warm, stop_stage=5):
    x = np.asarray(inp["x"], np.float32)
    Bn, T, Dm = x.shape
    nseg = T // seg
    n_cores = Bn * nseg
    n_main = seg // 512
    NT = n_warm + 1 + n_main
    nc = build_program(n_warm, n_main, stop_stage)
    sh = _shared_inputs(inp)
    in_maps = []
    for c in range(n_cores):
        bi, si = divmod(c, nseg)
        p0 = si * seg
        start = p0 - 512 * (n_warm + 1)
        xs = np.zeros((NT * 512, Dm), np.float32)
        lo = max(start, 0)
        xs[lo - start:] = x[bi, lo:p0 + seg]
        m = dict(sh)
        m["x"] = xs
        m["rot"] = _rot_table(np.maximum(np.arange(start, p0 + seg), 0))
        cols = sh["cols"].copy()
        o, _ = COLS["flag"]
        cols[:, o] = 0.0 if si == 0 else 1.0
        m["cols"] = cols
        in_maps.append(m)
    res = run_bass_kernel_spmd(nc, in_maps, core_ids=list(range(n_cores)))
    out = np.zeros((Bn, T, Dm), np.float32)
    for c in range(n_cores):
        bi, si = divmod(c, nseg)
        out[bi, si * seg:(si + 1) * seg] = res.results[c]["out"]
    return out


def kernel(**inputs):
    return run_module(inputs, seg=4096, n_warm=8, stop_stage=5)
```
